# Optimizing a Trainium2 kernel written in Bass

```python
import jax, jax.numpy as jnp
from jax import lax
import numpy as np

D_MODEL = 2048
BATCH = 8
SEQ = 4096
DEPTH = 1
DEC_BATCH = 8
DEC_SEQ = 16
PAST_LEN = 2048

CHUNK = 64
HEAD_DIM = 64
N_HEADS_A = 16
N_HEADS_B = 16
N_KV_B = 2
GQA_R = N_HEADS_B // N_KV_B
D_A = N_HEADS_A * HEAD_DIM
D_B = N_HEADS_B * HEAD_DIM
D_KV_B = N_KV_B * HEAD_DIM
D_IN = 3 * D_A + D_B + 2 * D_KV_B
A_PREV_CHUNKS = 8
A_REACH = A_PREV_CHUNKS * CHUNK
REL_CLIP = 128
B_WINDOW = 128
B_PREV_CHUNKS = B_WINDOW // CHUNK
ROPE_THETA = 500000.0
ROPE_DIM = HEAD_DIM // 4
D_FF = -(-8 * D_MODEL // (3 * 256)) * 256
NEG_INF = -1e30
EPS = 1e-6

kernel_name = 'hymba_chunk_band_swa_sink_encoder_step'


def rmsnorm(x, g):
    xf = x.astype(jnp.float32)
    y = xf * lax.rsqrt(jnp.mean(xf * xf, axis=-1, keepdims=True) + EPS)
    return (y * g.astype(jnp.float32)).astype(x.dtype)


def rope_partial(x, pos):
    half = ROPE_DIM // 2
    inv_freq = ROPE_THETA ** (-jnp.arange(half, dtype=jnp.float32) * 2.0 / ROPE_DIM)
    ang = pos.astype(jnp.float32)[:, None] * inv_freq[None, :]
    cos = jnp.cos(ang)[:, None, :]
    sin = jnp.sin(ang)[:, None, :]
    xf = x.astype(jnp.float32)
    x1 = xf[..., :half]
    x2 = xf[..., half:ROPE_DIM]
    out = jnp.concatenate([x1 * cos - x2 * sin, x2 * cos + x1 * sin, xf[..., ROPE_DIM:]], axis=-1)
    return out.astype(x.dtype)


def split_proj(p):
    b, s, _ = p.shape
    cuts = [D_A, 2 * D_A, 3 * D_A, 3 * D_A + D_B, 3 * D_A + D_B + D_KV_B]
    qa, ka, va, qb, kb, vb = jnp.split(p, cuts, axis=-1)
    return (qa.reshape(b, s, N_HEADS_A, HEAD_DIM), ka.reshape(b, s, N_HEADS_A, HEAD_DIM),
            va.reshape(b, s, N_HEADS_A, HEAD_DIM), qb.reshape(b, s, N_HEADS_B, HEAD_DIM),
            kb.reshape(b, s, N_KV_B, HEAD_DIM), vb.reshape(b, s, N_KV_B, HEAD_DIM))


def band_gather(t, n_prev):
    b, s, h, d = t.shape
    nc = s // CHUNK
    tp = jnp.pad(t.reshape(b, nc, CHUNK, h, d), ((0, 0), (n_prev, 0), (0, 0), (0, 0), (0, 0)))
    idx = jnp.arange(nc)[:, None] + jnp.arange(n_prev + 1)[None, :]
    return tp[:, idx].reshape(b, nc, (n_prev + 1) * CHUNK, h, d)


def band_valid(nc, n_prev):
    src = jnp.arange(nc)[:, None] - n_prev + jnp.arange(n_prev + 1)[None, :]
    return jnp.repeat(src >= 0, CHUNK, axis=1)


def rel_bias(table, qpos, kpos):
    rel = jnp.clip(qpos[:, None] - kpos[None, :], -REL_CLIP, REL_CLIP) + REL_CLIP
    return table.astype(jnp.float32)[:, rel][:, None]


def band_attend(q, k, v, valid, bias=None, sink=None):
    s = jnp.einsum('bgqhrd,bgkhd->bghrqk', q, k, preferred_element_type=jnp.float32) * (HEAD_DIM ** -0.5)
    if bias is not None:
        s = s + bias
    s = jnp.where(valid[None, :, None, None, None, :], s, NEG_INF)
    m = jnp.max(s, axis=-1, keepdims=True)
    if sink is not None:
        sk = sink.astype(jnp.float32)[None, None, :, :, None, None]
        m = jnp.maximum(m, sk)
        p = jnp.exp(s - m)
        denom = jnp.sum(p, axis=-1, keepdims=True) + jnp.exp(sk - m)
    else:
        p = jnp.exp(s - m)
        denom = jnp.sum(p, axis=-1, keepdims=True)
    w = (p / denom).astype(v.dtype)
    o = jnp.einsum('bghrqk,bgkhd->bgqhrd', w, v)
    b, g, nq = o.shape[:3]
    return o.reshape(b, g * nq, -1)


def merge_and_ffn(x, oa, ob, norm_grp_a, norm_grp_b, w_out, norm_ffn, w_gate, w_up, w_down):
    o = jnp.concatenate([rmsnorm(oa, norm_grp_a), rmsnorm(ob, norm_grp_b)], axis=-1)
    x = x + o @ w_out
    h = rmsnorm(x, norm_ffn)
    return x + (jax.nn.silu(h @ w_gate) * (h @ w_up)) @ w_down


def prompt_layer(x, w_in, norm_mix, rel_table, sinks, norm_grp_a, norm_grp_b, w_out,
                 norm_ffn, w_gate, w_up, w_down):
    b, s, _ = x.shape
    nc = s // CHUNK
    pos = jnp.arange(s, dtype=jnp.int32)
    qa, ka, va, qb, kb, vb = split_proj(rmsnorm(x, norm_mix) @ w_in)
    qb = rope_partial(qb, pos)
    kb = rope_partial(kb, pos)
    bias = rel_bias(rel_table, jnp.arange(CHUNK),
                    jnp.arange((A_PREV_CHUNKS + 1) * CHUNK) - A_PREV_CHUNKS * CHUNK)
    oa = band_attend(qa.reshape(b, nc, CHUNK, N_HEADS_A, 1, HEAD_DIM),
                     band_gather(ka, A_PREV_CHUNKS), band_gather(va, A_PREV_CHUNKS),
                     band_valid(nc, A_PREV_CHUNKS), bias=bias)
    ob = band_attend(qb.reshape(b, nc, CHUNK, N_KV_B, GQA_R, HEAD_DIM),
                     band_gather(kb, B_PREV_CHUNKS), band_gather(vb, B_PREV_CHUNKS),
                     band_valid(nc, B_PREV_CHUNKS), sink=sinks.reshape(N_KV_B, GQA_R))
    y = merge_and_ffn(x, oa, ob, norm_grp_a, norm_grp_b, w_out, norm_ffn, w_gate, w_up, w_down)
    keep_a = min(A_REACH, s)
    keep_b = min(B_WINDOW, s)
    return y, ka[:, s - keep_a:], va[:, s - keep_a:], kb[:, s - keep_b:], vb[:, s - keep_b:]


def sample_layer(x, ck_a, cv_a, ck_b, cv_b, w_in, norm_mix, rel_table, sinks, norm_grp_a,
                 norm_grp_b, w_out, norm_ffn, w_gate, w_up, w_down):
    b, s, _ = x.shape
    pos = PAST_LEN + jnp.arange(s, dtype=jnp.int32)
    qa, ka, va, qb, kb, vb = split_proj(rmsnorm(x, norm_mix) @ w_in)
    qb = rope_partial(qb, pos)
    kb = rope_partial(kb, pos)
    keep_a = ck_a.shape[1]
    keep_b = ck_b.shape[1]
    bias = rel_bias(rel_table, jnp.arange(s),
                    jnp.concatenate([jnp.arange(keep_a) - keep_a, jnp.arange(s)]))
    oa = band_attend(qa.reshape(b, 1, s, N_HEADS_A, 1, HEAD_DIM),
                     jnp.concatenate([ck_a, ka], axis=1)[:, None],
                     jnp.concatenate([cv_a, va], axis=1)[:, None],
                     jnp.ones((1, keep_a + s), dtype=bool), bias=bias)
    ob = band_attend(qb.reshape(b, 1, s, N_KV_B, GQA_R, HEAD_DIM),
                     jnp.concatenate([ck_b, kb], axis=1)[:, None],
                     jnp.concatenate([cv_b, vb], axis=1)[:, None],
                     jnp.ones((1, keep_b + s), dtype=bool), sink=sinks.reshape(N_KV_B, GQA_R))
    y = merge_and_ffn(x, oa, ob, norm_grp_a, norm_grp_b, w_out, norm_ffn, w_gate, w_up, w_down)
    return y, ka, va, kb, vb


def setup_inputs(seed: int = 0) -> dict:
    key = jax.random.key(seed)
    ks = jax.random.split(key, 20)
    f32 = jnp.float32
    keep_a = min(A_REACH, PAST_LEN)
    keep_b = min(B_WINDOW, PAST_LEN)

    def nrm(k, shape, scale):
        return jax.random.normal(k, shape, f32) * scale

    return {
        'x_prompt': nrm(ks[0], (BATCH, SEQ, D_MODEL), 1.0),
        'x_sample': nrm(ks[1], (DEC_BATCH, DEC_SEQ, D_MODEL), 1.0),
        'cache_a_k': nrm(ks[2], (DEPTH, DEC_BATCH, keep_a, N_HEADS_A, HEAD_DIM), 1.0),
        'cache_a_v': nrm(ks[3], (DEPTH, DEC_BATCH, keep_a, N_HEADS_A, HEAD_DIM), 1.0),
        'cache_b_k': nrm(ks[4], (DEPTH, DEC_BATCH, keep_b, N_KV_B, HEAD_DIM), 1.0),
        'cache_b_v': nrm(ks[5], (DEPTH, DEC_BATCH, keep_b, N_KV_B, HEAD_DIM), 1.0),
        'w_in': nrm(ks[6], (DEPTH, D_MODEL, D_IN), D_MODEL ** -0.5),
        'norm_mix': 1.0 + nrm(ks[7], (DEPTH, D_MODEL), 0.05),
        'rel_table': nrm(ks[8], (DEPTH, N_HEADS_A, 2 * REL_CLIP + 1), 0.1),
        'sinks': nrm(ks[9], (DEPTH, N_HEADS_B), 0.5),
        'norm_grp_a': 1.0 + nrm(ks[10], (DEPTH, D_A), 0.05),
        'norm_grp_b': 1.0 + nrm(ks[11], (DEPTH, D_B), 0.05),
        'w_out': nrm(ks[12], (DEPTH, D_A + D_B, D_MODEL), (D_A + D_B) ** -0.5),
        'norm_ffn': 1.0 + nrm(ks[13], (DEPTH, D_MODEL), 0.05),
        'w_gate': nrm(ks[14], (DEPTH, D_MODEL, D_FF), D_MODEL ** -0.5),
        'w_up': nrm(ks[15], (DEPTH, D_MODEL, D_FF), D_MODEL ** -0.5),
        'w_down': nrm(ks[16], (DEPTH, D_FF, D_MODEL), D_FF ** -0.5),
        'norm_final': 1.0 + nrm(ks[17], (D_MODEL,), 0.05),
    }


def reference(x_prompt, x_sample, cache_a_k, cache_a_v, cache_b_k, cache_b_v, w_in, norm_mix,
              rel_table, sinks, norm_grp_a, norm_grp_b, w_out, norm_ffn, w_gate, w_up, w_down,
              norm_final):
    xp = x_prompt
    xs = x_sample
    pa_k, pa_v, pb_k, pb_v = [], [], [], []
    sa_k, sa_v, sb_k, sb_v = [], [], [], []
    for l in range(DEPTH):
        w = (w_in[l], norm_mix[l], rel_table[l], sinks[l], norm_grp_a[l], norm_grp_b[l],
             w_out[l], norm_ffn[l], w_gate[l], w_up[l], w_down[l])
        xp, ak, av, bk, bv = prompt_layer(xp, *w)
        pa_k.append(ak); pa_v.append(av); pb_k.append(bk); pb_v.append(bv)
        xs, ak, av, bk, bv = sample_layer(xs, cache_a_k[l], cache_a_v[l], cache_b_k[l], cache_b_v[l], *w)
        sa_k.append(ak); sa_v.append(av); sb_k.append(bk); sb_v.append(bv)
    y_prompt = rmsnorm(xp, norm_final)
    y_sample = rmsnorm(xs, norm_final)
    return (y_prompt, y_sample,
            jnp.stack(pa_k), jnp.stack(pa_v), jnp.stack(pb_k), jnp.stack(pb_v),
            jnp.stack(sa_k), jnp.stack(sa_v), jnp.stack(sb_k), jnp.stack(sb_v))
```

```python
import bisect
import contextlib
import numpy as np
import concourse.bass as bass
import concourse.mybir as mybir
from concourse.bass_utils import run_bass_kernel_spmd

F32 = mybir.dt.float32
BF16 = mybir.dt.bfloat16
AF = mybir.ActivationFunctionType
ALU = mybir.AluOpType

D = 2048
KC = 16
DFF = 5632
NFF = 44
EPS = 1e-6
N_CORES = 8


class _Op:
    __slots__ = ("idx", "eng", "fn", "deps", "dma_key", "ticket", "signal")

    def __init__(self, idx, eng, fn, dma_key):
        self.idx = idx
        self.eng = eng
        self.fn = fn
        self.deps = []
        self.dma_key = dma_key
        self.ticket = None
        self.signal = False


class Sched:
    STREAMS = ("pe", "act", "dve", "pool", "sp")

    def __init__(self, nc):
        self.nc = nc
        self.ops = []
        self.last_w = {}
        self.readers = {}

    def op(self, eng, fn, reads=(), writes=(), dma_key=None):
        o = _Op(len(self.ops), eng, fn, dma_key)
        deps = {}
        compute = dma_key is None
        psr = [r for r in reads if isinstance(r, tuple) and r[0] == "ps"]
        if psr:
            reads = [r for r in reads if not (isinstance(r, tuple) and r[0] == "ps")]
            writes = list(writes) + psr

        def add(p, raw):
            same = compute and p.dma_key is None and p.eng == eng
            if same and eng == "pe":
                return
            deps[p.idx] = p

        for r in reads:
            for p in self.last_w.get(r, ()):
                add(p, True)
        for w in writes:
            for p in self.last_w.get(w, ()):
                add(p, False)
            for p in self.readers.get(w, {}).values():
                add(p, False)
        o.deps = list(deps.values())
        for r in reads:
            d = self.readers.setdefault(r, {})
            d[eng if compute else ("dma", o.idx)] = o
        for w in writes:
            self.last_w[w] = [o]
            self.readers[w] = {}
        self.ops.append(o)
        return o

    def emit(self):
        nc = self.nc
        seen = {s: {} for s in self.STREAMS}
        for o in self.ops:
            kept = []
            sd = seen[o.eng]
            for p in sorted(o.deps, key=lambda q: q.idx):
                if p.dma_key is None:
                    if sd.get(p.eng, -1) >= p.idx:
                        continue
                    sd[p.eng] = p.idx
                    kept = [k for k in kept if not (k.dma_key is None and k.eng == p.eng)]
                kept.append(p)
            o.deps = kept
            for p in kept:
                p.signal = True
        cnt = {}
        for o in self.ops:
            if o.dma_key is not None:
                k = ("dma", o.dma_key)
                cnt[k] = cnt.get(k, 0) + 16
                o.ticket = cnt[k]
            elif o.signal:
                k = ("eng", o.eng)
                cnt[k] = cnt.get(k, 0) + 1
                o.ticket = cnt[k]
        keys = list(cnt.keys())
        sems = {}
        dma_idx = {}
        for o in self.ops:
            if o.dma_key is not None:
                dma_idx.setdefault(o.dma_key, []).append(o.idx)
        with contextlib.ExitStack() as st:
            for k in keys:
                sems[k] = st.enter_context(nc.semaphore("s_%s_%s" % (k[0], str(k[1]))))
            block = st.enter_context(nc.Block())
            streams = {s: [o for o in self.ops if o.eng == s] for s in self.STREAMS}

            def run(stream, e):
                waited = {}
                for o in streams[stream]:
                    need = {}
                    for p in o.deps:
                        if p.dma_key is not None:
                            k = ("dma", p.dma_key)
                            lst = dma_idx[p.dma_key]
                            tk = 16 * bisect.bisect_left(lst, o.idx)
                        else:
                            k = ("eng", p.eng)
                            tk = p.ticket
                        if tk > need.get(k, 0):
                            need[k] = tk
                    for k, tk in need.items():
                        if waited.get(k, 0) >= tk:
                            continue
                        waited[k] = tk
                        e.wait_ge(sems[k], tk)
                    ins = o.fn(e)
                    if o.dma_key is not None:
                        ins.then_inc(sems[("dma", o.dma_key)], 16)
                    elif o.signal:
                        ins.then_inc(sems[("eng", o.eng)], 1)
                if stream == "sp":
                    for k in keys:
                        if k[0] == "dma":
                            e.wait_ge(sems[k], cnt[k])

            @block.tensor
            def _(e):
                run("pe", e)

            @block.scalar
            def _(e):
                run("act", e)

            @block.vector
            def _(e):
                run("dve", e)

            @block.gpsimd
            def _(e):
                run("pool", e)

            @block.sync
            def _(e):
                run("sp", e)


def _weight_groups():
    groups = []

    def g(name, parts, nk, gain, rows0=0):
        groups.append(dict(name=name, parts=parts, nk=nk, gain=gain, rows0=rows0))

    g("qa0", [("w_in", 0, 512, 0)], 16, "mix")
    g("qa1", [("w_in", 512, 512, 0)], 16, "mix")
    g("ka0", [("w_in", 1024, 512, 0)], 16, "mix")
    g("ka1", [("w_in", 1536, 512, 0)], 16, "mix")
    g("qb0", [("w_in", 3072, 512, 0)], 16, "mix")
    g("qb1", [("w_in", 3584, 512, 0)], 16, "mix")
    g("kvb", [("w_in", 4096, 256, 0)], 16, "mix")
    g("va0", [("w_in", 2048, 512, 0)], 16, "mix")
    g("va1", [("w_in", 2560, 512, 0)], 16, "mix")
    for c in range(4):
        g("wo%d" % c, [("w_out", 512 * c, 512, 0)], 16, "out")
    for j in range(22):
        g("gu%d" % j, [("w_gate", 256 * j, 256, 0), ("w_up", 256 * j, 256, 256)], 16, "ffn")
    for cg in range(4):
        for rg in range(4):
            g("dn%d_%d" % (cg, rg), [("w_down", 512 * cg, 512, 0)], 11, None, rows0=rg * 11)
    return groups


GROUPS = _weight_groups()
GIDX = {g["name"]: i for i, g in enumerate(GROUPS)}


class TileDesc:
    def __init__(self, t, nsub, P, nqc, cq, last, kind, row0):
        self.t = t
        self.nsub = nsub
        self.P = P
        self.nqc = nqc
        self.cq = cq
        self.ntok = (nsub - 1) * 128 + P
        self.last = last
        self.kind = kind
        self.row0 = row0


def build(NT=8, with_sample=True):
    nc = bass.Bass("TRN2", target_bir_lowering=False)

    def din(name, shape):
        return nc.dram_tensor(name, shape, F32, kind="ExternalInput").ap()

    def dout(name, shape):
        return nc.dram_tensor(name, shape, F32, kind="ExternalOutput").ap()

    SEQ = NT * 512
    x_prompt = din("x_prompt", [SEQ, D])
    x_sample = din("x_sample", [16, D])
    cache_a_k = din("cache_a_k", [512, 1024])
    cache_a_v = din("cache_a_v", [512, 1024])
    cache_b_k = din("cache_b_k", [128, 128])
    cache_b_v = din("cache_b_v", [128, 128])
    W = {
        "w_in": din("w_in", [D, 4352]),
        "w_out": din("w_out", [D, D]),
        "w_gate": din("w_gate", [D, DFF]),
        "w_up": din("w_up", [D, DFF]),
        "w_down": din("w_down", [DFF, D]),
    }
    norm_mix = din("norm_mix", [D])
    rel_table = din("rel_table", [16, 257])
    sinks = din("sinks", [1, 16])
    norm_grp = din("norm_grp", [D])
    norm_ffn = din("norm_ffn", [D])
    norm_final = din("norm_final", [1, D])
    ident_d = din("ident", [128, 128])
    ropec_d = din("ropec", [128, 33 * 8])
    ropes_d = din("ropes", [128, 33 * 8])

    y_prompt = dout("y_prompt", [SEQ, D])
    y_sample = dout("y_sample", [16, D])
    pak = dout("pak", [512, 1024])
    pav = dout("pav", [512, 1024])
    pbk = dout("pbk", [128, 128])
    pbv = dout("pbv", [128, 128])
    sak = dout("sak", [16, 1024])
    sav = dout("sav", [16, 1024])
    sbk = dout("sbk", [16, 128])
    sbv = dout("sbv", [16, 128])

    NG = len(GROUPS)
    wsc = nc.dram_tensor("wsc", [NG, 128, 8192], BF16, kind="Internal").ap()
    ext_d = nc.dram_tensor("ext", [16, 512], F32, kind="Internal").ap()

    S = Sched(nc)
    with contextlib.ExitStack() as st:
        def sb(name, shape, dt):
            return st.enter_context(nc.sbuf_tensor("sb_" + name, shape, dt))

        xbuf = sb("xbuf", [128, 4, D], F32)
        uni = sb("uni", [128, 44 * 512], BF16)
        bufA = sb("bufA", [128, KC, 512], BF16)
        ka = sb("ka", [128, 8, 1024], BF16)
        va = sb("va", [128, 8, 16, 64], BF16)
        kbr = sb("kbr", [128, 2, 1024], BF16)
        vbr = sb("vbr", [128, 8, 2, 64], BF16)
        expB = sb("expB", [128, 16, 256], BF16)
        xstage = sb("xstage", [128, D], F32)
        xnb = sb("xnb", [128, D], BF16)
        wsl = [sb("wsl%d" % i, [128, KC, 512], BF16) for i in range(2)]
        recs = [sb("recs%d" % i, [128, 512], F32) for i in range(2)]
        recs2 = [sb("recsb%d" % i, [128, 512], F32) for i in range(2)]
        onesv = sb("onesv", [128, 64], BF16)
        onesv16 = sb("onesv16", [128, 64], BF16)
        maskB = sb("maskB", [128, 4, 64], BF16)
        esink2 = sb("esink2", [128, 8], F32)
        outst = [sb("outst%d" % i, [128, 512], F32) for i in range(2)]
        sgt = [sb("sgt%d" % i, [128, 512], BF16) for i in range(2)]
        identb = sb("identb", [128, 128], BF16)
        ident32 = sb("ident32", [128, 128], F32)
        ones32 = sb("ones32", [128, 128], F32)
        kbo = sb("kbo", [128, 128], F32)
        dummy = sb("dummyt", [128, 1], F32)
        onesb = sb("onesb", [128, 1], BF16)
        ropec = sb("ropec", [128, 33, 8], F32)
        ropes = sb("ropes", [128, 33, 8], F32)
        gains = sb("gains", [128, 3, 16], F32)
        cbias = sb("cbias", [128, 16], F32)
        negc = sb("negc", [128, 16], F32)
        esink = sb("esink", [128, 16], F32)
        epsb = sb("epsb", [128, 1], F32)
        negone = sb("negone", [128, 1], F32)
        stt = sb("stt", [128, 64], F32)
        stt2 = sb("stt2", [128, 32], F32)
        ropet = sb("ropet", [128, 4, 18, 8], F32)
        ext_sb = sb("ext_sb", [16, 512], F32)

        bufB = uni[:, 0:8192].rearrange("p (c n) -> p c n", n=512)
        qa = uni[:, 8192:12288].rearrange("p (c n) -> p c n", n=512)
        qb = uni[:, 12288:16384].rearrange("p (c n) -> p c n", n=512)
        NPT = 4
        PT = [uni[:, 16384 + 1024 * j:16384 + 1024 * (j + 1)].rearrange("p (e n) -> p e n", n=512) for j in range(NPT)]
        qkr_all = uni[:, 0:5120].rearrange("p (s n) -> p s n", n=1280)
        rst = uni[:, 5120:7424].bitcast(F32).rearrange("p (s h d) -> p s h d", h=18, d=16)
        sqc = [uni[:, 16384 + 512 * j:16384 + 512 * (j + 1)] for j in range(2)]
        actT = uni[:, :].rearrange("p (c n) -> p c n", n=512)
        stage32 = [xbuf[:, :, :].rearrange("p a b -> p (a b)").rearrange("p (k n) -> p k n", n=512),
                   uni[:, 0:16384].bitcast(F32).rearrange("p (k n) -> p k n", n=512)]
        bt = bufA[:, :, :].rearrange("p a b -> p (a b)").bitcast(F32).rearrange("p (h n) -> p h n", n=256)
        ystage = [uni[:, 0:4096].bitcast(F32), uni[:, 4096:8192].bitcast(F32)]
        gfin = xstage[:, :]

        def UNI(a, b):
            return [("uni", c) for c in range(a, b)]
        R_bufB = UNI(0, 16)
        R_stage = [[("xb", s) for s in range(4)], UNI(0, 32)]
        R_bufA = [("bufA", s) for s in range(4)]

        psS = st.enter_context(nc.psum_tensor("psS", [128, 4, 512], F32))
        psT = st.enter_context(nc.psum_tensor("psT", [128, 4, 512], F32))
        ps = [psS[:, b, :] for b in range(4)] + [psT[:, b, :] for b in range(4)]
        S_PAIRS = [(psS, 0), (psS, 2), (psT, 2)]
        srot = {"i": 0}

        def psb(b):
            return ps[b].bitcast(BF16)

        def cload(dst, src, res, **kw):
            S.op("sp", lambda e: e.dma_start(out=dst, in_=src, **kw), writes=[res], dma_key="c_" + res)

        cload(ident32[:], ident_d, "ident32")
        cload(ropec[:].rearrange("p a b -> p (a b)"), ropec_d, "ropec")
        cload(ropes[:].rearrange("p a b -> p (a b)"), ropes_d, "ropes")
        cload(gains[:, 0, :], norm_mix.rearrange("(k p) -> p k", p=128), "g0", allow_slow_non_contiguous=True)
        cload(gains[:, 1, :], norm_grp.rearrange("(k p) -> p k", p=128), "g1", allow_slow_non_contiguous=True)
        cload(gains[:, 2, :], norm_ffn.rearrange("(k p) -> p k", p=128), "g2", allow_slow_non_contiguous=True)
        cload(esink[:], sinks.partition_broadcast(128), "esink0")
        cload(cbias[:], rel_table[:, 256:257].rearrange("h o -> o h").partition_broadcast(128), "cbias",
              allow_slow_non_contiguous=True)
        cload(ext_sb[:, 0:257], rel_table, "ext_a")
        S.op("act", lambda e: e.activation(out=esink[:], in_=esink[:], func=AF.Exp), reads=["esink0"], writes=["esink"])
        S.op("dve", lambda e: e.tensor_scalar(out=negc[:], in0=cbias[:], scalar1=-1.0, scalar2=None, op0=ALU.mult),
             reads=["cbias"], writes=["negc"])
        S.op("dve", lambda e: e.tensor_copy(out=identb[:], in_=ident32[:]), reads=["ident32"], writes=["identb"])
        S.op("pool", lambda e: e.memset(ones32[:], 1.0), writes=["ones32"])
        S.op("pool", lambda e: e.memset(onesb[:], 1.0), writes=["onesb"])
        S.op("pool", lambda e: e.memset(epsb[:], EPS), writes=["epsb"])
        S.op("pool", lambda e: e.memset(negone[:], -1.0), writes=["negone"])
        S.op("pool", lambda e: e.memset(va[:].rearrange("p a b c -> p (a b c)"), 0.0), writes=[("va", s, hh) for s in range(8) for hh in range(2)])
        S.op("pool", lambda e: e.memset(vbr[:].rearrange("p a b c -> p (a b c)"), 0.0), writes=[("vb", s) for s in range(8)])
        S.op("pool", lambda e: e.memset(ka[:].rearrange("p a b -> p (a b)"), 0.0), writes=[("ka", s, hh) for s in range(8) for hh in range(2)])
        S.op("pool", lambda e: e.memset(kbr[:].rearrange("p a b -> p (a b)"), 0.0), writes=[("kb", s) for s in range(8)])
        S.op("pool", lambda e: e.memset(onesv[:], 1.0), writes=["onesv"])
        S.op("pool", lambda e: e.memset(onesv16[:], 0.0), writes=["onesv16a"])
        S.op("pool", lambda e: e.memset(onesv16[0:16, :], 1.0), reads=["onesv16a"], writes=["onesv16"])
        S.op("pool", lambda e: e.memset(maskB[:].rearrange("p a b -> p (a b)"), 1.0), writes=["maskBa"])
        S.op("pool", lambda e: e.memset(maskB[64:128, 0, :], 0.0), reads=["maskBa"], writes=["maskBb"])
        S.op("pool", lambda e: e.memset(maskB[0:64, 3, :], 0.0), reads=["maskBa"], writes=["maskBc"])
        R_maskB = ["maskBa", "maskBb", "maskBc"]
        sk2 = sinks.rearrange("o (i e) -> o i e", e=2)
        cload(esink2[0:64, :], sk2[:, :, 0].partition_broadcast(64), "esk0", allow_slow_non_contiguous=True)
        cload(esink2[64:128, :], sk2[:, :, 1].partition_broadcast(64), "esk1", allow_slow_non_contiguous=True)
        S.op("act", lambda e: e.activation(out=esink2[:], in_=esink2[:], func=AF.Exp), reads=["esk0", "esk1"], writes=["esink2"])
        S.op("dve", lambda e: e.tensor_copy(out=ext_sb[:, 257:512], in_=ext_sb[:, 256:257].to_broadcast([16, 255])),
             reads=["ext_a"], writes=["ext_b"])
        S.op("sp", lambda e: e.dma_start(out=ext_d, in_=ext_sb[:]), reads=["ext_a", "ext_b"], writes=["ext_d"], dma_key="ext")
        for k in range(128):
            q = "sp" if k % 2 == 0 else "pool"
            S.op(q, lambda e, k=k: e.dma_start(out=bt[k:k + 1, :, :], in_=ext_d[:, 128 - k:384 - k].unsqueeze(0)),
                 reads=["ext_d"], writes=[("bt", k)], dma_key="bt%d" % (k % 2))
        for h in range(16):
            S.op("act", lambda e, h=h: e.activation(out=expB[:, h, :], in_=bt[:, h, :], func=AF.Exp, bias=negc[:, h:h + 1]),
                 reads=[("bt", k) for k in range(128)] + ["negc"], writes=[("expB", h)])
        S.op("pool", lambda e: e.memset(expB[64:128, :, 0:64], 0.0), reads=[], writes=[("expB", h) for h in range(16)])
        R_expB = [("expB", h) for h in range(16)]

        GSEL = {"mix": 0, "out": 1, "ffn": 2}
        cvc = {"i": 0}
        SQ = [xbuf[:, 1, :].rearrange("p (k n) -> p k n", n=512), xbuf[:, 2, :].rearrange("p (k n) -> p k n", n=512),
              xbuf[:, 3, :].rearrange("p (k n) -> p k n", n=512), xstage[:, :].rearrange("p (k n) -> p k n", n=512)]
        R_SQ = [("xb", 1), ("xb", 2), ("xb", 3), "xstage"]
        sqr = {"i": 0}

        def WK(slot):
            return [("wk", slot, k, dc) for k in range(16) for dc in (0, 256)]

        def prep_load_q(wname, rows0, k0, k1, c0, ncols):
            i = sqr["i"] % 4
            sqr["i"] += 1
            src = W[wname].rearrange("(k p) n -> p k n", p=128)
            S.op("sp", lambda e: e.dma_start(out=SQ[i][:, 0:k1 - k0, 0:ncols], in_=src[:, rows0 + k0:rows0 + k1, c0:c0 + ncols]),
                 reads=[], writes=[R_SQ[i]], dma_key="sq%d" % i)
            return i

        def prep_convert(slot, k, dcol, i, kk, scol, n, gain):
            use_act = (cvc["i"] % 3 == 2)
            cvc["i"] += 1
            dst = wsl[slot][:, k, dcol:dcol + n]
            srcv = SQ[i][:, kk, scol:scol + n]
            wrs = [("wk", slot, k, dcol)] if n <= 256 else [("wk", slot, k, 0), ("wk", slot, k, 256)]
            rds = [R_SQ[i]]
            if gain is None:
                if use_act:
                    S.op("act", lambda e: e.copy(out=dst, in_=srcv), reads=rds, writes=wrs)
                else:
                    S.op("dve", lambda e: e.tensor_copy(out=dst, in_=srcv), reads=rds, writes=wrs)
            else:
                gs = GSEL[gain]
                if use_act:
                    S.op("act", lambda e: e.activation(out=dst, in_=srcv, func=AF.Copy, scale=gains[:, gs, k:k + 1]),
                         reads=rds + ["g%d" % gs], writes=wrs)
                else:
                    S.op("dve", lambda e: e.tensor_scalar(out=dst, in0=srcv, scalar1=gains[:, gs, k:k + 1], scalar2=None, op0=ALU.mult),
                         reads=rds + ["g%d" % gs], writes=wrs)

        def prep_store(gi, slot, nk):
            S.op("pool", lambda e: e.dma_start(out=wsc[gi, :, 0:nk * 512], in_=wsl[slot][:, 0:nk, :].rearrange("p k n -> p (k n)")),
                 reads=WK(slot), writes=[("wsc", gi)], dma_key="pw%d" % slot)

        def prep_single(gi, slot):
            g = GROUPS[gi]
            nk = g["nk"]
            (wname, c0, ncols, dcol) = g["parts"][0]
            for q in range((nk + 3) // 4):
                k0, k1 = 4 * q, min(nk, 4 * q + 4)
                i = prep_load_q(wname, g["rows0"], k0, k1, c0, ncols)
                for k in range(k0, k1):
                    prep_convert(slot, k, 0, i, k - k0, 0, ncols, g["gain"])
            prep_store(gi, slot, nk)

        def prep_gu_pair(gi, slot):
            i2 = int(GROUPS[gi]["name"][2:]) // 2
            for q in range(4):
                ia = prep_load_q("w_gate", 0, 4 * q, 4 * q + 4, 512 * i2, 512)
                ib = prep_load_q("w_up", 0, 4 * q, 4 * q + 4, 512 * i2, 512)
                for k in range(4 * q, 4 * q + 4):
                    for j in range(2):
                        sl = slot if j == 0 else 1 - slot
                        prep_convert(sl, k, 0, ia, k - 4 * q, 256 * j, 256, "ffn")
                        prep_convert(sl, k, 256, ib, k - 4 * q, 256 * j, 256, "ffn")
            prep_store(gi, slot, 16)
            prep_store(gi + 1, 1 - slot, 16)

        S.op("pool", lambda e: e.memset(dummy[:], 0.0), reads=[],
             writes=[("bt", k) for k in range(128)] + [("bufA", s_) for s_ in range(4)])

        wstate = {"n": 0, "fused": True}
        prepared = {}

        def load_group2(name):
            gi = GIDX[name]
            nk = GROUPS[gi]["nk"]
            slot = wstate["n"] % 2
            wstate["n"] += 1
            if wstate["fused"] and gi not in prepared:
                if name.startswith("gu"):
                    assert int(name[2:]) % 2 == 0
                    prep_gu_pair(gi, slot)
                    prepared[gi] = slot
                    prepared[gi + 1] = 1 - slot
                else:
                    prep_single(gi, slot)
                    prepared[gi] = None
                return slot
            if wstate["fused"] and prepared.get(gi) is not None:
                assert prepared[gi] == slot
                prepared[gi] = None
                return slot
            S.op("sp", lambda e, gi=gi, slot=slot, nk=nk: e.dma_start(
                out=wsl[slot][:, 0:nk, :].rearrange("p k n -> p (k n)"), in_=wsc[gi, :, 0:nk * 512]),
                reads=[("wsc", gi)], writes=WK(slot), dma_key="w%d" % slot)
            return slot

        def wres(slot):
            return WK(slot)

        bank_rr = {"i": 0}

        def nt_pre(td, s, from_stage):
            P = td.P
            c0 = (4 * s) % 16 + (16 if from_stage else 0)
            ssum, lnv, rstd = stt[:, c0:c0 + 1], stt[:, c0 + 1:c0 + 2], stt[:, c0 + 2:c0 + 3]
            rn = ("stt", c0)
            if from_stage:
                if td.kind == "prompt":
                    srcd = x_prompt[td.row0 + s * 128:td.row0 + s * 128 + 128, :]
                else:
                    srcd = x_sample[0:16, :]
                S.op("sp", lambda e: e.dma_start(out=xstage[0:P, :], in_=srcd), writes=["xstage"], dma_key="xs")
                src, rsrc = xstage[0:P, :], "xstage"
            else:
                src, rsrc = xbuf[0:P, s, :], ("xb", s)
            S.op("act", lambda e: e.activation(out=xnb[0:P, :], in_=src, func=AF.Square, accum_out=ssum[0:P, :]),
                 reads=[rsrc], writes=["xnb", (rn, 0)])
            S.op("act", lambda e: e.activation(out=lnv[0:P, :], in_=ssum[0:P, :], func=AF.Ln, scale=1.0 / D, bias=epsb[0:P, :]),
                 reads=[(rn, 0), "epsb"], writes=[(rn, 1)])
            S.op("act", lambda e: e.activation(out=rstd[0:P, :], in_=lnv[0:P, :], func=AF.Exp, scale=-0.5),
                 reads=[(rn, 1)], writes=[(rn, 2)])
            S.op("dve", lambda e: e.tensor_scalar(out=xnb[0:P, :], in0=src, scalar1=rstd[0:P, :], scalar2=None, op0=ALU.mult),
                 reads=[rsrc, (rn, 2)], writes=["xnb"])

        def nt_pe(td, s):
            P = td.P
            for half in range(2):
                b = 6 + half
                for j in range(8):
                    kc = 8 * half + j
                    S.op("pe", lambda e, b=b, j=j, kc=kc: e.transpose(out=psb(b)[:, j * 128:j * 128 + P],
                                                                     in_=xnb[0:P, kc * 128:(kc + 1) * 128],
                                                                     identity=identb[0:P, 0:P]),
                         reads=["xnb", "identb"], writes=[("ps", b)])
                src = psb(b).rearrange("p (j n) -> p j n", n=128)[:, :, 0:P]
                dst = bufA[:, 8 * half:8 * half + 8, s * 128:s * 128 + P]
                evac("act" if half == 0 else "dve", dst, src, [("ps", b)], [("bufA", s)])

        def norm_and_transpose(td, from_stage):
            for s in range(td.nsub):
                nt_pre(td, s, from_stage)
                nt_pe(td, s)

        def evac(eng, dst, src, reads, writes):
            if eng == "act":
                S.op("act", lambda e: e.copy(out=dst, in_=src), reads=reads, writes=writes)
            else:
                S.op("dve", lambda e: e.tensor_copy(out=dst, in_=src), reads=reads, writes=writes)

        def transposes_tok_to_feat(td, s, src_bf, nblk, dsts, bank, rsrc):
            P = td.P
            for j in range(nblk):
                S.op("pe", lambda e, j=j, P=P: e.transpose(out=psb(bank)[:, j * 128:j * 128 + P],
                                                          in_=src_bf[0:P, j * 128:(j + 1) * 128], identity=identb[0:P, 0:P]),
                     reads=rsrc + ["identb"], writes=[("ps", bank)])

        def tile_program(td, td_next):
            t, P, ntok = td.t, td.P, td.ntok
            ev = {"i": 0}

            def nexteng():
                ev["i"] += 1
                return "act" if ev["i"] % 2 == 0 else "dve"

            for s in range(td.nsub):
                if td.kind == "prompt":
                    src = x_prompt[td.row0 + s * 128:td.row0 + s * 128 + 128, :]
                else:
                    src = x_sample[0:16, :]
                wr = [("xb", s)]
                S.op("pool", lambda e, s=s, src=src, P=P: e.dma_start(out=xbuf[0:P, s, :], in_=src), writes=wr, dma_key="x%d" % s)

            if td.kind == "sample":
                for blk in range(4):
                    stg = outst[blk % 2]
                    for half in range(2):
                        S.op("sp", lambda e, blk=blk, half=half, stg=stg: e.dma_start(
                            out=stg[:, :], in_=cache_a_k[blk * 128:(blk + 1) * 128, half * 512:(half + 1) * 512]),
                            writes=[("outst", blk % 2)], dma_key="cl%d" % (blk % 2))
                        S.op("dve", lambda e, half=half, stg=stg: e.tensor_copy(out=xnb[:, half * 512:(half + 1) * 512], in_=stg[:, :]),
                             reads=[("outst", blk % 2)], writes=["xnb"])
                    for j in range(8):
                        S.op("pe", lambda e, j=j: e.transpose(out=psb(6)[:, j * 128:(j + 1) * 128], in_=xnb[:, j * 128:(j + 1) * 128],
                                                             identity=identb[:, :]),
                             reads=["xnb", "identb"], writes=[("ps", 6)])
                    S.op("act", lambda e, blk=blk: e.copy(out=ka[:, :, blk * 128:(blk + 1) * 128],
                                                         in_=psb(6).rearrange("p (j n) -> p j n", n=128)),
                         reads=[("ps", 6)], writes=[("ka", blk, 0), ("ka", blk, 1)])
                    for half in range(2):
                        S.op("sp", lambda e, blk=blk, half=half, stg=stg: e.dma_start(
                            out=stg[:, :], in_=cache_a_v[blk * 128:(blk + 1) * 128, half * 512:(half + 1) * 512]),
                            writes=[("outst", blk % 2)], dma_key="cl%d" % (blk % 2))
                        S.op("dve", lambda e, blk=blk, half=half, stg=stg: e.tensor_copy(
                            out=va[:, blk, 8 * half:8 * half + 8, :], in_=stg[:, :].rearrange("p (h d) -> p h d", d=64)),
                            reads=[("outst", blk % 2)], writes=[("va", blk, half)])
                stg = outst[0]
                S.op("sp", lambda e: e.dma_start(out=outst[0][:, 0:128], in_=cache_b_k), writes=[("outst", 0)], dma_key="cl0")
                S.op("sp", lambda e: e.dma_start(out=outst[0][:, 128:256], in_=cache_b_v), writes=[("outst", 0)], dma_key="cl0")
                kin = outst[0][:, 0:128].rearrange("p (g d) -> p g d", d=64)
                S.op("dve", lambda e, kin=kin: e.tensor_copy(out=xnb[:, 0:256].rearrange("p (g r d) -> p g r d", r=2, d=64)[:, :, 0, :], in_=kin),
                     reads=[("outst", 0)], writes=["xnb"])
                S.op("dve", lambda e, kin=kin: e.tensor_copy(out=xnb[:, 0:256].rearrange("p (g r d) -> p g r d", r=2, d=64)[:, :, 1, :], in_=kin),
                     reads=[("outst", 0)], writes=["xnb"])
                for j in range(2):
                    S.op("pe", lambda e, j=j: e.transpose(out=psb(6)[:, j * 128:(j + 1) * 128], in_=xnb[:, j * 128:(j + 1) * 128],
                                                         identity=identb[:, :]),
                         reads=["xnb", "identb"], writes=[("ps", 6)])
                S.op("act", lambda e: e.copy(out=kbr[:, :, 3 * 128:4 * 128], in_=psb(6)[:, 0:256].rearrange("p (j n) -> p j n", n=128)),
                     reads=[("ps", 6)], writes=[("kb", 3)])
                vin = outst[0][:, 128:256].rearrange("p (g d) -> p g d", d=64)
                S.op("dve", lambda e, vin=vin: e.tensor_copy(out=vbr[:, 3, :, :], in_=vin), reads=[("outst", 0)], writes=[("vb", 3)])
                S.op("pool", lambda e: e.memset(va[:, 4, :, :].rearrange("p a b -> p (a b)"), 0.0), writes=[("va", 4, 0), ("va", 4, 1)])
                S.op("pool", lambda e: e.memset(vbr[:, 4, :, :].rearrange("p a b -> p (a b)"), 0.0), writes=[("vb", 4)])
                S.op("pool", lambda e: e.memset(ka[:, :, 512:640], 0.0), writes=[("ka", 4, 0), ("ka", 4, 1)])
                S.op("pool", lambda e: e.memset(kbr[:, :, 512:640], 0.0), writes=[("kb", 4)])

            if td.kind == "sample":
                norm_and_transpose(td, False)

            rbufA = [("bufA", s) for s in range(td.nsub)]
            deferred = []
            gslot = {}

            def fm_unit(gname, oc, b):
                if gname not in gslot:
                    gslot[gname] = load_group2(gname)
                slot = gslot[gname]
                for kc in range(KC):
                    S.op("pe", lambda e, kc=kc: e.matmul(
                        ps[b][:, 0:ntok], lhsT=wsl[slot][:, kc, oc * 128:(oc + 1) * 128], rhs=bufA[:, kc, 0:ntok],
                        start=(kc == 0), stop=(kc == KC - 1)),
                        reads=wres(slot) + rbufA, writes=[("ps", b)])
                hh = 0 if gname[2] == "0" else 1
                pair = 4 * hh + oc
                if gname.startswith("qa"):
                    evac(nexteng(), qa[:, pair, 0:ntok], ps[b][:, 0:ntok], [("ps", b)], [("uni", 16 + pair)])
                else:
                    col0 = (t % 2) * 512
                    slots_w = [("ka", (4 * t + s) % 8, hh) for s in range(td.nsub)]
                    evac(nexteng(), ka[:, pair, col0:col0 + ntok], ps[b][:, 0:ntok], [("ps", b)], slots_w)

            def next_bank4():
                b = bank_rr["i"] % 4
                bank_rr["i"] += 1
                return b

            dbank = {"i": 0}

            def next_dbank():
                tns, b0 = S_PAIRS[srot["i"] % 3]
                srot["i"] += 1
                return b0 if tns is psS else 4 + b0

            for gname in ("qa0", "ka0"):
                for oc in range(4):
                    fm_unit(gname, oc, next_bank4())
            for gname in ("qa1", "ka1"):
                for oc in range(4):
                    deferred.append(lambda gname=gname, oc=oc: fm_unit(gname, oc, next_dbank()))

            tm_groups = ["qb0", "qb1", "kvb"] + (["ka0", "ka1"] if td.last else []) + ["va0", "va1"]
            co = {"i": 0}

            def out_store(dst_ap, src_ps, b, ncols):
                i = co["i"] % 2
                co["i"] += 1
                S.op("act", lambda e, i=i: e.copy(out=outst[i][0:P, 0:ncols], in_=src_ps), reads=[("ps", b)], writes=[("outst", i)])
                S.op("pool", lambda e, i=i: e.dma_start(out=dst_ap, in_=outst[i][0:P, 0:ncols]), reads=[("outst", i)], dma_key="co%d" % i)

            def tm_unit(gname, s, b):
                if (gname, "tm") not in gslot:
                    gslot[(gname, "tm")] = load_group2(gname)
                slot = gslot[(gname, "tm")]
                ncols = 256 if gname == "kvb" else 512
                for kc in range(KC):
                    S.op("pe", lambda e, b=b, slot=slot, kc=kc, s=s, ncols=ncols: e.matmul(
                        ps[b][0:P, 0:ncols], lhsT=bufA[:, kc, s * 128:s * 128 + P], rhs=wsl[slot][:, kc, 0:ncols],
                        start=(kc == 0), stop=(kc == KC - 1)),
                        reads=wres(slot) + [("bufA", s)], writes=[("ps", b)])
                blk = (4 * t + s) % 8
                RQK = R_bufB
                if gname in ("qb0", "qb1"):
                    g = int(gname[2])
                    acq = (s == 0 and g == 0)
                    S.op("act", lambda e, b=b, s=s, g=g: e.copy(out=qkr_all[0:P, s, g * 512:(g + 1) * 512], in_=ps[b][0:P, 0:512]),
                         reads=[("ps", b)] + ([] if acq else RQK), writes=[("qkq", s, g)] + (RQK if acq else []))
                    S.op("dve", lambda e, b=b, s=s, g=g: e.tensor_copy(
                        out=rst[0:P, s, 8 * g:8 * g + 8, :],
                        in_=ps[b][0:P, 0:512].rearrange("p (h d) -> p h d", d=64)[:, :, 0:16]),
                        reads=[("ps", b)] + RQK, writes=[("rst", s, g)])
                elif gname == "kvb":
                    kps = ps[b][0:P, 0:128].rearrange("p (g d) -> p g d", d=64)
                    S.op("dve", lambda e, s=s, kps=kps: e.tensor_copy(out=rst[0:P, s, 16:18, :], in_=kps[:, :, 0:16]),
                         reads=[("ps", b)] + RQK, writes=[("rst", s, 2)])
                    kd = qkr_all[0:P, s, 1024:1280].rearrange("p (g r d) -> p g r d", r=2, d=64)
                    S.op("act", lambda e, kd=kd, kps=kps: e.copy(out=kd[:, :, 0, :], in_=kps), reads=[("ps", b)] + RQK, writes=[("qkk", s, 0)])
                    S.op("act", lambda e, kd=kd, kps=kps: e.copy(out=kd[:, :, 1, :], in_=kps), reads=[("ps", b)] + RQK, writes=[("qkk", s, 1)])
                    vin = ps[b][0:P, 128:256].rearrange("p (g d) -> p g d", d=64)
                    S.op("act", lambda e, vin=vin, blk=blk: e.copy(out=vbr[0:P, blk, :, :], in_=vin),
                         reads=[("ps", b)], writes=[("vb", blk)])
                    want_out = td.last and (td.kind == "sample" or s == 3)
                    if want_out:
                        out_store(sbv[0:16, :] if td.kind == "sample" else pbv[:, :], ps[b][0:P, 128:256], b, 128)
                        S.op("act", lambda e, b=b: e.copy(out=kbo[0:P, :], in_=ps[b][0:P, 0:128]), reads=[("ps", b)], writes=["kbo"])
                    rq = [("rst", s, 0), ("rst", s, 1), ("rst", s, 2)]
                    x1, x2 = rst[0:P, s, :, 0:8], rst[0:P, s, :, 8:16]
                    jcol = (4 * t + s) if td.kind == "prompt" else 32
                    cs = ropec[0:P, jcol:jcol + 1, :].to_broadcast([P, 18, 8])
                    sn = ropes[0:P, jcol:jcol + 1, :].to_broadcast([P, 18, 8])
                    T = [ropet[0:P, i, :, :] for i in range(4)]
                    S.op("dve", lambda e, x1=x1, cs=cs, T=T: e.tensor_tensor(out=T[0], in0=x1, in1=cs, op=ALU.mult),
                         reads=rq + ["ropec"] + RQK, writes=[("ropet", 0)])
                    S.op("dve", lambda e, x2=x2, sn=sn, T=T: e.tensor_tensor(out=T[1], in0=x2, in1=sn, op=ALU.mult),
                         reads=rq + ["ropes"] + RQK, writes=[("ropet", 1)])
                    S.op("dve", lambda e, x2=x2, cs=cs, T=T: e.tensor_tensor(out=T[2], in0=x2, in1=cs, op=ALU.mult),
                         reads=rq + ["ropec"] + RQK, writes=[("ropet", 2)])
                    S.op("dve", lambda e, x1=x1, sn=sn, T=T: e.tensor_tensor(out=T[3], in0=x1, in1=sn, op=ALU.mult),
                         reads=rq + ["ropes"] + RQK, writes=[("ropet", 3)])
                    qv = qkr_all[0:P, s, 0:1024].rearrange("p (h d) -> p h d", d=64)
                    rqq = [("qkq", s, 0), ("qkq", s, 1)]
                    S.op("dve", lambda e, qv=qv, T=T: e.tensor_tensor(out=qv[:, :, 0:8], in0=T[0][:, 0:16, :], in1=T[1][:, 0:16, :], op=ALU.subtract),
                         reads=[("ropet", 0), ("ropet", 1)] + rqq + RQK, writes=[("qkq", s, 2)])
                    S.op("dve", lambda e, qv=qv, T=T: e.tensor_tensor(out=qv[:, :, 8:16], in0=T[2][:, 0:16, :], in1=T[3][:, 0:16, :], op=ALU.add),
                         reads=[("ropet", 2), ("ropet", 3)] + rqq + RQK, writes=[("qkq", s, 3)])
                    S.op("dve", lambda e, x1=x1, T=T: e.tensor_tensor(out=x1[:, 16:18, :], in0=T[0][:, 16:18, :], in1=T[1][:, 16:18, :], op=ALU.subtract),
                         reads=[("ropet", 0), ("ropet", 1), ("ropet", 3)] + RQK, writes=[("rstk", s, 0)])
                    S.op("dve", lambda e, x2=x2, T=T: e.tensor_tensor(out=x2[:, 16:18, :], in0=T[2][:, 16:18, :], in1=T[3][:, 16:18, :], op=ALU.add),
                         reads=[("ropet", 2), ("ropet", 3), ("ropet", 1)] + RQK, writes=[("rstk", s, 1)])
                    rkk = [("rstk", s, 0), ("rstk", s, 1)]
                    for r in range(2):
                        S.op("dve", lambda e, kd=kd, s=s, r=r: e.tensor_copy(out=kd[:, :, r, 0:16], in_=rst[0:P, s, 16:18, :]),
                             reads=rkk + [("qkk", s, r)] + RQK, writes=[("qkk", s, 2 + r)])
                    if want_out:
                        S.op("dve", lambda e, s=s: e.tensor_copy(out=kbo[0:P, :].rearrange("p (g d) -> p g d", d=64)[:, :, 0:16],
                                                                in_=rst[0:P, s, 16:18, :]),
                             reads=rkk + ["kbo"] + RQK, writes=["kbo2"])
                        dstk = sbk[0:16, :] if td.kind == "sample" else pbk[:, :]
                        S.op("pool", lambda e, dstk=dstk: e.dma_start(out=dstk, in_=kbo[0:P, :]), reads=["kbo", "kbo2"], dma_key="cok")
                    rqr = [("qkq", s, i) for i in range(4)] + [("qkk", s, i) for i in range(4)]
                    for j in range(8):
                        S.op("pe", lambda e, j=j, s=s: e.transpose(out=psb(6)[:, j * 128:j * 128 + P], in_=qkr_all[0:P, s, j * 128:(j + 1) * 128],
                                                                  identity=identb[0:P, 0:P]),
                             reads=rqr + ["identb"] + RQK, writes=[("ps", 6)])
                    S.op("act", lambda e, s=s: e.copy(out=qb[:, :, s * 128:s * 128 + P],
                                                     in_=psb(6).rearrange("p (j n) -> p j n", n=128)[:, :, 0:P]),
                         reads=[("ps", 6)], writes=[("uni", 24 + i) for i in range(8)] if s == 0 else [("qb", s)])
                    for j in range(2):
                        S.op("pe", lambda e, j=j, s=s: e.transpose(out=psb(7)[:, j * 128:j * 128 + P],
                                                                  in_=qkr_all[0:P, s, 1024 + j * 128:1024 + (j + 1) * 128],
                                                                  identity=identb[0:P, 0:P]),
                             reads=rqr + ["identb"] + RQK, writes=[("ps", 7)])
                    kcol = (t % 2) * 512 + s * 128
                    S.op("dve", lambda e, kcol=kcol: e.tensor_copy(out=kbr[:, :, kcol:kcol + P],
                                                                  in_=psb(7)[:, 0:256].rearrange("p (j n) -> p j n", n=128)[:, :, 0:P]),
                         reads=[("ps", 7)], writes=[("kb", blk)])
                elif gname in ("ka0", "ka1"):
                    g = int(gname[2])
                    dst = (sak[0:16, g * 512:(g + 1) * 512] if td.kind == "sample"
                           else pak[s * 128:(s + 1) * 128, g * 512:(g + 1) * 512])
                    out_store(dst, ps[b][0:P, 0:512], b, 512)
                else:
                    g = int(gname[2])
                    srcv = ps[b][0:P, 0:512].rearrange("p (h d) -> p h d", d=64)
                    dstv = va[0:P, blk, 8 * g:8 * g + 8, :]
                    evac(nexteng(), dstv, srcv, [("ps", b)], [("va", blk, g)])
                    if td.last:
                        dst = (sav[0:16, g * 512:(g + 1) * 512] if td.kind == "sample"
                               else pav[s * 128:(s + 1) * 128, g * 512:(g + 1) * 512])
                        out_store(dst, ps[b][0:P, 0:512], b, 512)


            for gname in tm_groups:
                for s in range(td.nsub):
                    if gname == "va1":
                        deferred.append(lambda gname=gname, s=s: tm_unit(gname, s, next_dbank()))
                    else:
                        b6 = bank_rr["i"] % 6
                        bank_rr["i"] += 1
                        tm_unit(gname, s, b6)

            rqb_all = [("uni", 24 + i) for i in range(8)] + [("qb", s) for s in range(1, td.nsub)]
            blocks = []
            pair_order = [("B", p_) for p_ in range(8)] + [("A", p_) for p_ in range(8)]
            n_single = len(deferred)
            pair_ip = {kp: i_ for i_, kp in enumerate(pair_order)}
            for (kind, pair) in pair_order:
                if True:
                    if kind == "A":
                        kbs = range(max(0, 4 * t - 4), 4 * t + 4)
                        reach = 9
                    else:
                        kbs = range(max(0, 4 * t - 1), 4 * t + 4)
                        reach = 3
                    lst = []
                    for kb in kbs:
                        clo = max(0, 2 * kb - 8 * t)
                        chi = min(td.nqc - 1, 2 * kb + reach - 8 * t)
                        if clo > chi:
                            continue
                        lst.append((kb, clo, chi))
                    for i, (kb, clo, chi) in enumerate(lst):
                        blocks.append(dict(kind=kind, pair=pair, kb=kb, clo=clo, chi=chi,
                                           first=(i == 0), lastb=(i == len(lst) - 1)))
            nblk = len(blocks)

            def emit_S(bi):
                B_ = blocks[bi]
                kind, pair, kb, clo, chi = B_["kind"], B_["pair"], B_["kb"], B_["clo"], B_["chi"]
                nq = (chi - clo) * 64 + td.cq
                q0 = clo * 64
                slot = kb % 8
                stn, sbl = S_PAIRS[srot["i"] % 3]
                srot["i"] += 1
                sb0 = sbl if stn is psS else 4 + sbl
                B_["pt"] = bi % NPT
                B_["nq"], B_["q0"] = nq, q0
                for e_ in range(2):
                    lo, hi = 64 * e_, 64 * e_ + 64
                    if kind == "A":
                        S.op("pe", lambda e, lo=lo, hi=hi, e_=e_: e.matmul(ps[sb0 + e_][:, 0:nq], lhsT=ka[lo:hi, pair, slot * 128:(slot + 1) * 128],
                                                                          rhs=qa[lo:hi, pair, q0:q0 + nq], start=True, stop=True),
                             reads=[("ka", slot, pair // 4), ("uni", 16 + pair)], writes=[("ps", sb0 + e_)])
                    else:
                        kv = pair // 4
                        S.op("pe", lambda e, lo=lo, hi=hi, e_=e_, kv=kv: e.matmul(ps[sb0 + e_][:, 0:nq], lhsT=kbr[lo:hi, kv, slot * 128:(slot + 1) * 128],
                                                                                 rhs=qb[lo:hi, pair, q0:q0 + nq], start=True, stop=True),
                             reads=[("kb", slot)] + rqb_all, writes=[("ps", sb0 + e_)])
                pt = PT[B_["pt"]]
                rpts = [("uni", 32 + 2 * B_["pt"]), ("uni", 33 + 2 * B_["pt"])]
                m_lo = 8 * t + clo - 2 * kb
                m_hi = 8 * t + chi - 2 * kb
                S.op("act", lambda e: e.activation(out=pt[:, :, 0:nq], in_=stn[:, sbl:sbl + 2, 0:nq], func=AF.Exp, scale=0.125),
                     reads=[("ps", sb0), ("ps", sb0 + 1)], writes=rpts)
                if kind == "A":
                    if m_lo <= 3:
                        nb = min(3, m_hi) - m_lo + 1
                        ncol = (nb - 1) * 64 + td.cq
                        S.op("dve", lambda e: e.tensor_tensor(out=pt[:, :, 0:ncol], in0=pt[:, :, 0:ncol],
                                                              in1=expB[:, 2 * pair:2 * pair + 2, m_lo * 64:m_lo * 64 + ncol], op=ALU.mult),
                             reads=rpts + R_expB, writes=rpts)
                    if m_hi == 9:
                        c0 = (chi - clo) * 64
                        S.op("dve", lambda e: e.tensor_tensor(out=pt[:, :, c0:c0 + td.cq], in0=pt[:, :, c0:c0 + td.cq],
                                                              in1=maskB[:, 3:4, 0:td.cq].to_broadcast([128, 2, td.cq]), op=ALU.mult),
                             reads=rpts + R_maskB, writes=rpts)
                else:
                    mflat = maskB[:].rearrange("p a b -> p (a b)")[:, m_lo * 64:m_lo * 64 + nq]
                    S.op("dve", lambda e: e.tensor_tensor(out=pt[:, :, 0:nq], in0=pt[:, :, 0:nq],
                                                          in1=mflat.unsqueeze(1).to_broadcast([128, 2, nq]), op=ALU.mult),
                         reads=rpts + R_maskB, writes=rpts)

            def emit_PV(bi):
                B_ = blocks[bi]
                kind, pair, kb = B_["kind"], B_["pair"], B_["kb"]
                nq, q0 = B_["nq"], B_["q0"]
                slot = kb % 8
                pt = PT[B_["pt"]]
                rpts = [("uni", 32 + 2 * B_["pt"]), ("uni", 33 + 2 * B_["pt"])]
                ip_ = pair_ip[(kind, pair)]
                nb_ = 4
                db_ = nb_ + 1
                ov = onesv16 if (td.kind == "sample" and kb == 4) else onesv
                for e_ in range(2):
                    lo = 64 * e_
                    if kind == "A":
                        vsrc, rv = va[:, slot, 2 * pair + e_, :], ("va", slot, pair // 4)
                    else:
                        vsrc, rv = vbr[:, slot, pair // 4, :], ("vb", slot)
                    S.op("pe", lambda e, lo=lo, e_=e_, vsrc=vsrc: e.matmul(ps[nb_][lo:lo + 64, q0:q0 + nq], lhsT=vsrc, rhs=pt[:, e_, 0:nq],
                                                                         start=B_["first"], stop=B_["lastb"], skip_group_check=True),
                         reads=[rv] + rpts, writes=[("ps", nb_)])
                for e_ in range(2):
                    lo = 64 * e_
                    S.op("pe", lambda e, lo=lo, e_=e_: e.matmul(ps[db_][lo:lo + 64, q0:q0 + nq], lhsT=ov[:, :], rhs=pt[:, e_, 0:nq],
                                                               start=B_["first"], stop=B_["lastb"], skip_group_check=True),
                         reads=rpts + ["onesv", "onesv16"], writes=[("ps", db_)])
                if B_["lastb"]:
                    finalize(kind, pair, nb_, db_)

            fin_ctr = {"i": 0}

            def finalize(kind, pair, nb_, db_):
                chunk = pair + (8 if kind == "B" else 0)
                ri = fin_ctr["i"] % 2
                fin_ctr["i"] += 1
                rc = recs[ri]
                rc2 = recs2[ri]
                if kind == "B":
                    S.op("act", lambda e: e.activation(out=rc[:, 0:ntok], in_=ps[db_][:, 0:ntok], func=AF.Ln, bias=esink2[:, pair:pair + 1]),
                         reads=[("ps", db_), "esink2"], writes=[("recs", ri)])
                else:
                    S.op("act", lambda e: e.activation(out=rc[:, 0:ntok], in_=ps[db_][:, 0:ntok], func=AF.Ln),
                         reads=[("ps", db_)], writes=[("recs", ri)])
                S.op("act", lambda e: e.activation(out=rc2[:, 0:ntok], in_=rc[:, 0:ntok], func=AF.Exp, scale=-1.0),
                     reads=[("recs", ri)], writes=[("recs2", ri)])
                S.op("dve", lambda e: e.tensor_tensor(out=bufB[:, chunk, 0:ntok], in0=ps[nb_][:, 0:ntok], in1=rc2[:, 0:ntok], op=ALU.mult),
                     reads=[("ps", nb_), ("recs2", ri)], writes=[("uni", chunk)])
                if deferred:
                    deferred.pop(0)()

            LOOK = 2
            for bi in range(nblk + LOOK):
                if bi < nblk:
                    emit_S(bi)
                if bi - LOOK >= 0:
                    emit_PV(bi - LOOK)

            first_mm = True
            for c in range(16):
                sq = sqc[c % 2]
                rs = ("uni", 32 + c % 2)
                S.op("act", lambda e, c=c, sq=sq: e.activation(out=sq[:, 0:ntok], in_=bufB[:, c, 0:ntok], func=AF.Square),
                     reads=[("uni", c)], writes=[rs])
                for s in range(td.nsub):
                    col = (c // 8) * 4 + s
                    S.op("pe", lambda e, sq=sq, s=s, col=col, fm=first_mm: e.matmul(
                        ps[6][0:P, col:col + 1], lhsT=sq[:, s * 128:s * 128 + P], rhs=onesb[:, 0:1],
                        start=fm, stop=(c == 15 and s == td.nsub - 1), skip_group_check=True),
                        reads=[rs, "onesb"], writes=[("ps", 6)])
                    first_mm = False
            lnr = stt[:, 48:56]
            rr = stt[:, 56:64]
            if td.nsub < 4:
                pass
            for g in range(2):
                S.op("act", lambda e, g=g: e.activation(out=lnr[0:P, 4 * g:4 * g + td.nsub], in_=ps[6][0:P, 4 * g:4 * g + td.nsub],
                                                       func=AF.Ln, scale=1.0 / 1024, bias=epsb[0:P, :]),
                     reads=[("ps", 6), "epsb"], writes=[("lnr", g)])
                S.op("act", lambda e, g=g: e.activation(out=rr[0:P, 4 * g:4 * g + td.nsub], in_=lnr[0:P, 4 * g:4 * g + td.nsub],
                                                       func=AF.Exp, scale=-0.5),
                     reads=[("lnr", g)], writes=[("rr", g)])
            def ffn_norm_sub(s):
                rstd = stt2[:, 16 + 4 * s + 2:16 + 4 * s + 3]
                S.op("dve", lambda e: e.tensor_scalar(out=xnb[0:P, :], in0=xbuf[0:P, s, :], scalar1=rstd[0:P, :],
                                                      scalar2=None, op0=ALU.mult),
                     reads=[("xb", s), (("stt2n", s), 2)], writes=["xnb"])
                nt_pe(td, s)

            for cg in range(4):
                slot = load_group2("wo%d" % cg)
                for s in range(td.nsub):
                    if cg == 3 and s >= 2:
                        ffn_norm_sub(s - 2)
                    pb = 2 * (bank_rr["i"] % 3)
                    bank_rr["i"] += 1
                    for half in range(2):
                        for k in range(8):
                            kc = 8 * half + k
                            S.op("pe", lambda e, pb=pb, half=half, kc=kc, k=k, s=s, slot=slot: e.matmul(
                                ps[pb + half][0:P, 0:512], lhsT=bufB[:, kc, s * 128:s * 128 + P], rhs=wsl[slot][:, kc, 0:512],
                                start=(k == 0), stop=(k == 7)),
                                reads=wres(slot) + [("uni", kc)], writes=[("ps", pb + half)])
                    for half in range(2):
                        S.op("dve", lambda e, pb=pb, half=half, s=s, cg=cg: e.scalar_tensor_tensor(
                            out=xbuf[0:P, s, cg * 512:(cg + 1) * 512], in0=ps[pb + half][0:P, 0:512],
                            scalar=rr[0:P, 4 * half + s:4 * half + s + 1], in1=xbuf[0:P, s, cg * 512:(cg + 1) * 512],
                            op0=ALU.mult, op1=ALU.add),
                            reads=[("ps", pb + half), ("rr", half), ("xb", s)], writes=[("xb", s)])
                    sgj = sgt[(4 * cg + s) % 2]
                    S.op("act", lambda e, s=s, cg=cg, sgj=sgj: e.activation(
                        out=sgj[0:P, :], in_=xbuf[0:P, s, cg * 512:(cg + 1) * 512], func=AF.Square,
                        accum_out=stt2[0:P, 4 * s + cg:4 * s + cg + 1]),
                        reads=[("xb", s)], writes=[("sgt", (4 * cg + s) % 2), ("stt2", s, cg)])
                    if cg == 3:
                        c0 = 16 + 4 * s
                        ssum, lnv, rstd = stt2[:, c0:c0 + 1], stt2[:, c0 + 1:c0 + 2], stt2[:, c0 + 2:c0 + 3]
                        rn2 = ("stt2n", s)
                        S.op("dve", lambda e, s=s, ssum=ssum: e.reduce_sum(out=ssum[0:P, :], in_=stt2[0:P, 4 * s:4 * s + 4],
                                                                          axis=mybir.AxisListType.X),
                             reads=[("stt2", s, c) for c in range(4)], writes=[(rn2, 0)])
                        S.op("act", lambda e, ssum=ssum, lnv=lnv: e.activation(out=lnv[0:P, :], in_=ssum[0:P, :], func=AF.Ln,
                                                                              scale=1.0 / D, bias=epsb[0:P, :]),
                             reads=[(rn2, 0), "epsb"], writes=[(rn2, 1)])
                        S.op("act", lambda e, rstd=rstd, lnv=lnv: e.activation(out=rstd[0:P, :], in_=lnv[0:P, :], func=AF.Exp, scale=-0.5),
                             reads=[(rn2, 1)], writes=[(rn2, 2)])

            for s in range(max(0, td.nsub - 2), td.nsub):
                ffn_norm_sub(s)

            for j in range(22):
                slot = load_group2("gu%d" % j)
                for f in range(2):
                    ffc = 2 * j + f
                    pb = 2 * (bank_rr["i"] % 4)
                    bank_rr["i"] += 1
                    for which in range(2):
                        for kc in range(KC):
                            c0 = 256 * which + 128 * f
                            S.op("pe", lambda e, pb=pb, which=which, kc=kc, c0=c0, slot=slot: e.matmul(
                                ps[pb + which][:, 0:ntok], lhsT=wsl[slot][:, kc, c0:c0 + 128], rhs=bufA[:, kc, 0:ntok],
                                start=(kc == 0), stop=(kc == KC - 1)),
                                reads=wres(slot) + rbufA, writes=[("ps", pb + which)])
                    sg = sgt[ffc % 2]
                    S.op("act", lambda e, pb=pb, sg=sg: e.activation(out=sg[:, 0:ntok], in_=ps[pb][:, 0:ntok], func=AF.Silu),
                         reads=[("ps", pb)], writes=[("sgt", ffc % 2)])
                    S.op("dve", lambda e, pb=pb, sg=sg, ffc=ffc: e.tensor_tensor(out=actT[:, ffc, 0:ntok], in0=ps[pb + 1][:, 0:ntok],
                                                                               in1=sg[:, 0:ntok], op=ALU.mult),
                         reads=[("ps", pb + 1), ("sgt", ffc % 2)], writes=[("uni", ffc), ("qb", 1), ("qb", 2), ("qb", 3)] if ffc in range(24, 32) else [("uni", ffc)])

            R_gfin = ["xstage"]
            gf_done = {"v": False}

            def load_gfin():
                gf_done["v"] = True
                S.op("pool", lambda e: e.dma_start(out=gfin[:, :], in_=norm_final.partition_broadcast(128)), writes=R_gfin, dma_key="gf")

            gidx = 0
            if td_next is not None:
                nt_pre(td_next, 0, True)
            for cg in range(4):
                banks = [0, 1, 2, 3] if cg % 2 == 0 else [4, 5, 2, 3]
                for rg in range(4):
                    slot = load_group2("dn%d_%d" % (cg, rg))
                    for i in range(11):
                        ffc = rg * 11 + i
                        for s in range(td.nsub):
                            S.op("pe", lambda e, s=s, i=i, ffc=ffc, slot=slot, bk=banks[s]: e.matmul(
                                ps[bk][0:P, 0:512], lhsT=actT[:, ffc, s * 128:s * 128 + P], rhs=wsl[slot][:, i, 0:512],
                                start=(ffc == 0), stop=(ffc == NFF - 1)),
                                reads=wres(slot) + [("uni", ffc)], writes=[("ps", banks[s])])
                    if td_next is not None:
                        sn_ = gidx - 2
                        if 0 <= sn_ < td_next.nsub:
                            nt_pe(td_next, sn_)
                            if sn_ + 1 < td_next.nsub:
                                nt_pre(td_next, sn_ + 1, True)
                            else:
                                load_gfin()
                    gidx += 1
                for s in (2, 3, 0, 1):
                    if s >= td.nsub:
                        continue
                    S.op("dve", lambda e, s=s, cg=cg, bk=banks[s]: e.tensor_tensor(
                        out=xbuf[0:P, s, cg * 512:(cg + 1) * 512], in0=ps[bk][0:P, 0:512],
                        in1=xbuf[0:P, s, cg * 512:(cg + 1) * 512], op=ALU.add),
                        reads=[("ps", banks[s]), ("xb", s)], writes=[("xb", s)])
            if not gf_done["v"]:
                load_gfin()
            for s in range(td.nsub):
                c0 = 32 + 4 * s
                ssum, lnv, rstd = stt[:, c0:c0 + 1], stt[:, c0 + 1:c0 + 2], stt[:, c0 + 2:c0 + 3]
                rn = ("stt", c0)
                ys = ystage[s % 2]
                ry = [("uni", c) for c in range(8 * (s % 2), 8 * (s % 2) + 8)]
                S.op("act", lambda e, s=s, ssum=ssum, ys=ys: e.activation(out=ys[0:P, :], in_=xbuf[0:P, s, :], func=AF.Square,
                                                                         accum_out=ssum[0:P, :]),
                     reads=[("xb", s)], writes=ry + [(rn, 0)])
                S.op("act", lambda e, ssum=ssum, lnv=lnv: e.activation(out=lnv[0:P, :], in_=ssum[0:P, :], func=AF.Ln,
                                                                      scale=1.0 / D, bias=epsb[0:P, :]),
                     reads=[(rn, 0), "epsb"], writes=[(rn, 1)])
                S.op("act", lambda e, rstd=rstd, lnv=lnv: e.activation(out=rstd[0:P, :], in_=lnv[0:P, :], func=AF.Exp, scale=-0.5),
                     reads=[(rn, 1)], writes=[(rn, 2)])
                S.op("dve", lambda e, s=s, rstd=rstd, ys=ys: e.scalar_tensor_tensor(
                    out=ys[0:P, :], in0=xbuf[0:P, s, :], scalar=rstd[0:P, :], in1=gfin[0:P, :], op0=ALU.mult, op1=ALU.mult),
                    reads=[("xb", s), (rn, 2)] + R_gfin, writes=ry)
                dst = (y_prompt[td.row0 + s * 128:td.row0 + s * 128 + 128, :] if td.kind == "prompt" else y_sample[0:16, :])
                S.op("pool", lambda e, ys=ys, dst=dst: e.dma_start(out=dst, in_=ys[0:P, :]), reads=ry,
                     dma_key="y%d" % (s % 2))

        ptiles = [TileDesc(t, 4, 128, 8, 64, t == NT - 1, "prompt", t * 512) for t in range(NT)]
        if with_sample:
            tile_program(TileDesc(1, 1, 16, 1, 16, True, "sample", 0), None)
        wstate["fused"] = False
        assert len(prepared) == NG or not with_sample
        norm_and_transpose(ptiles[0], True)
        for i, td in enumerate(ptiles):
            tile_program(td, ptiles[i + 1] if i + 1 < len(ptiles) else None)

        with nc.allow_low_precision(reason="bf16 matmul operands, fp32 accumulation"):
            S.emit()
    return nc


def _consts():
    half = 8
    inv_freq = (np.float32(500000.0) ** (-(np.arange(half, dtype=np.float32) * np.float32(2.0) / np.float32(16)))).astype(np.float32)
    pos = np.zeros((128, 33), np.float32)
    for j in range(32):
        pos[:, j] = 128 * j + np.arange(128)
    pos[:, 32] = 2048 + np.arange(128)
    ang = (pos[:, :, None] * inv_freq[None, None, :]).astype(np.float32)
    return (np.eye(128, dtype=np.float32), np.cos(ang).astype(np.float32).reshape(128, 33 * 8),
            np.sin(ang).astype(np.float32).reshape(128, 33 * 8))


_NC_CACHE = {}


def make_in_map(i, inp, NT=8):
    ident, rc, rs = _consts()
    f = lambda a: np.ascontiguousarray(a, dtype=np.float32)
    return {
        "x_prompt": f(inp["x_prompt"][i][:NT * 512]),
        "x_sample": f(inp["x_sample"][i]),
        "cache_a_k": f(inp["cache_a_k"][0, i].reshape(512, 1024)),
        "cache_a_v": f(inp["cache_a_v"][0, i].reshape(512, 1024)),
        "cache_b_k": f(inp["cache_b_k"][0, i].reshape(128, 128)),
        "cache_b_v": f(inp["cache_b_v"][0, i].reshape(128, 128)),
        "w_in": f(inp["w_in"][0]),
        "w_out": f(inp["w_out"][0]),
        "w_gate": f(inp["w_gate"][0]),
        "w_up": f(inp["w_up"][0]),
        "w_down": f(inp["w_down"][0]),
        "norm_mix": f(inp["norm_mix"][0]),
        "rel_table": f(inp["rel_table"][0]),
        "sinks": f(inp["sinks"][0].reshape(1, 16)),
        "norm_grp": f(np.concatenate([inp["norm_grp_a"][0], inp["norm_grp_b"][0]])),
        "norm_ffn": f(inp["norm_ffn"][0]),
        "norm_final": f(inp["norm_final"].reshape(1, D)),
        "ident": ident, "ropec": rc, "ropes": rs,
    }


def kernel(**inputs):
    inp = {k: np.asarray(v) for k, v in inputs.items()}
    if "nc" not in _NC_CACHE:
        _NC_CACHE["nc"] = build(8, True)
    nc = _NC_CACHE["nc"]
    in_maps = [make_in_map(i, inp) for i in range(N_CORES)]
    res = run_bass_kernel_spmd(nc, in_maps, core_ids=list(range(N_CORES)))
    R = res.results

    def stack(name, shape):
        return np.stack([np.asarray(R[i][name], dtype=np.float32).reshape(shape) for i in range(N_CORES)])

    y_prompt = stack("y_prompt", (4096, D))
    y_sample = stack("y_sample", (16, D))
    pak = stack("pak", (512, 16, 64))[None]
    pav = stack("pav", (512, 16, 64))[None]
    pbk = stack("pbk", (128, 2, 64))[None]
    pbv = stack("pbv", (128, 2, 64))[None]
    sak = stack("sak", (16, 16, 64))[None]
    sav = stack("sav", (16, 16, 64))[None]
    sbk = stack("sbk", (16, 2, 64))[None]
    sbv = stack("sbv", (16, 2, 64))[None]
    return (y_prompt, y_sample, pak, pav, pbk, pbv, sak, sav, sbk, sbv)
```

```python
import bisect
import contextlib
import numpy as np
import concourse.bass as bass
import concourse.mybir as mybir
from concourse.bass_utils import run_bass_kernel_spmd

F32 = mybir.dt.float32
BF16 = mybir.dt.bfloat16
AF = mybir.ActivationFunctionType
ALU = mybir.AluOpType

D = 2048
KC = 16
DFF = 5632
NFF = 44
EPS = 1e-6
N_CORES = 8


class _Op:
    __slots__ = ("idx", "eng", "fn", "deps", "dma_key", "ticket", "signal")

    def __init__(self, idx, eng, fn, dma_key):
        self.idx = idx
        self.eng = eng
        self.fn = fn
        self.deps = []
        self.dma_key = dma_key
        self.ticket = None
        self.signal = False


class Sched:
    STREAMS = ("pe", "act", "dve", "pool", "sp")

    def __init__(self, nc):
        self.nc = nc
        self.ops = []
        self.last_w = {}
        self.readers = {}

    def op(self, eng, fn, reads=(), writes=(), dma_key=None):
        o = _Op(len(self.ops), eng, fn, dma_key)
        deps = {}
        compute = dma_key is None
        psr = [r for r in reads if isinstance(r, tuple) and r[0] == "ps"]
        if psr:
            reads = [r for r in reads if not (isinstance(r, tuple) and r[0] == "ps")]
            writes = list(writes) + psr

        def add(p, raw):
            same = compute and p.dma_key is None and p.eng == eng
            if same and eng == "pe":
                return
            deps[p.idx] = p

        for r in reads:
            for p in self.last_w.get(r, ()):
                add(p, True)
        for w in writes:
            for p in self.last_w.get(w, ()):
                add(p, False)
            for p in self.readers.get(w, {}).values():
                add(p, False)
        o.deps = list(deps.values())
        for r in reads:
            d = self.readers.setdefault(r, {})
            d[eng if compute else ("dma", o.idx)] = o
        for w in writes:
            self.last_w[w] = [o]
            self.readers[w] = {}
        self.ops.append(o)
        return o

    def emit(self):
        nc = self.nc
        seen = {s: {} for s in self.STREAMS}
        for o in self.ops:
            kept = []
            sd = seen[o.eng]
            for p in sorted(o.deps, key=lambda q: q.idx):
                if p.dma_key is None:
                    if sd.get(p.eng, -1) >= p.idx:
                        continue
                    sd[p.eng] = p.idx
                    kept = [k for k in kept if not (k.dma_key is None and k.eng == p.eng)]
                kept.append(p)
            o.deps = kept
            for p in kept:
                p.signal = True
        cnt = {}
        for o in self.ops:
            if o.dma_key is not None:
                k = ("dma", o.dma_key)
                cnt[k] = cnt.get(k, 0) + 16
                o.ticket = cnt[k]
            elif o.signal:
                k = ("eng", o.eng)
                cnt[k] = cnt.get(k, 0) + 1
                o.ticket = cnt[k]
        keys = list(cnt.keys())
        sems = {}
        dma_idx = {}
        for o in self.ops:
            if o.dma_key is not None:
                dma_idx.setdefault(o.dma_key, []).append(o.idx)
        with contextlib.ExitStack() as st:
            for k in keys:
                sems[k] = st.enter_context(nc.semaphore("s_%s_%s" % (k[0], str(k[1]))))
            block = st.enter_context(nc.Block())
            streams = {s: [o for o in self.ops if o.eng == s] for s in self.STREAMS}

            def run(stream, e):
                waited = {}
                for o in streams[stream]:
                    need = {}
                    for p in o.deps:
                        if p.dma_key is not None:
                            k = ("dma", p.dma_key)
                            lst = dma_idx[p.dma_key]
                            tk = 16 * bisect.bisect_left(lst, o.idx)
                        else:
                            k = ("eng", p.eng)
                            tk = p.ticket
                        if tk > need.get(k, 0):
                            need[k] = tk
                    for k, tk in need.items():
                        if waited.get(k, 0) >= tk:
                            continue
                        waited[k] = tk
                        e.wait_ge(sems[k], tk)
                    ins = o.fn(e)
                    if o.dma_key is not None:
                        ins.then_inc(sems[("dma", o.dma_key)], 16)
                    elif o.signal:
                        ins.then_inc(sems[("eng", o.eng)], 1)
                if stream == "sp":
                    for k in keys:
                        if k[0] == "dma":
                            e.wait_ge(sems[k], cnt[k])

            @block.tensor
            def _(e):
                run("pe", e)

            @block.scalar
            def _(e):
                run("act", e)

            @block.vector
            def _(e):
                run("dve", e)

            @block.gpsimd
            def _(e):
                run("pool", e)

            @block.sync
            def _(e):
                run("sp", e)


def _weight_groups():
    groups = []

    def g(name, parts, nk, gain, rows0=0):
        groups.append(dict(name=name, parts=parts, nk=nk, gain=gain, rows0=rows0))

    g("qa0", [("w_in", 0, 512, 0)], 16, "mix")
    g("qa1", [("w_in", 512, 512, 0)], 16, "mix")
    g("ka0", [("w_in", 1024, 512, 0)], 16, "mix")
    g("ka1", [("w_in", 1536, 512, 0)], 16, "mix")
    g("qb0", [("w_in", 3072, 512, 0)], 16, "mix")
    g("qb1", [("w_in", 3584, 512, 0)], 16, "mix")
    g("kvb", [("w_in", 4096, 256, 0)], 16, "mix")
    g("va0", [("w_in", 2048, 512, 0)], 16, "mix")
    g("va1", [("w_in", 2560, 512, 0)], 16, "mix")
    for c in range(4):
        g("wo%d" % c, [("w_out", 512 * c, 512, 0)], 16, "out")
    for j in range(22):
        g("gu%d" % j, [("w_gate", 256 * j, 256, 0), ("w_up", 256 * j, 256, 256)], 16, "ffn")
    for cg in range(4):
        for rg in range(4):
            g("dn%d_%d" % (cg, rg), [("w_down", 512 * cg, 512, 0)], 11, None, rows0=rg * 11)
    return groups


GROUPS = _weight_groups()
GIDX = {g["name"]: i for i, g in enumerate(GROUPS)}


class TileDesc:
    def __init__(self, t, nsub, P, nqc, cq, last, kind, row0):
        self.t = t
        self.nsub = nsub
        self.P = P
        self.nqc = nqc
        self.cq = cq
        self.ntok = (nsub - 1) * 128 + P
        self.last = last
        self.kind = kind
        self.row0 = row0


def build(NT=8, with_sample=True):
    nc = bass.Bass("TRN2", target_bir_lowering=False)

    def din(name, shape):
        return nc.dram_tensor(name, shape, F32, kind="ExternalInput").ap()

    def dout(name, shape):
        return nc.dram_tensor(name, shape, F32, kind="ExternalOutput").ap()

    SEQ = NT * 512
    x_prompt = din("x_prompt", [SEQ, D])
    x_sample = din("x_sample", [16, D])
    cache_a_k = din("cache_a_k", [512, 1024])
    cache_a_v = din("cache_a_v", [512, 1024])
    cache_b_k = din("cache_b_k", [128, 128])
    cache_b_v = din("cache_b_v", [128, 128])
    W = {
        "w_in": din("w_in", [D, 4352]),
        "w_out": din("w_out", [D, D]),
        "w_gate": din("w_gate", [D, DFF]),
        "w_up": din("w_up", [D, DFF]),
        "w_down": din("w_down", [DFF, D]),
    }
    norm_mix = din("norm_mix", [D])
    rel_table = din("rel_table", [16, 257])
    sinks = din("sinks", [1, 16])
    norm_grp = din("norm_grp", [D])
    norm_ffn = din("norm_ffn", [D])
    norm_final = din("norm_final", [1, D])
    ident_d = din("ident", [128, 128])
    ropec_d = din("ropec", [128, 33 * 8])
    ropes_d = din("ropes", [128, 33 * 8])

    y_prompt = dout("y_prompt", [SEQ, D])
    y_sample = dout("y_sample", [16, D])
    pak = dout("pak", [512, 1024])
    pav = dout("pav", [512, 1024])
    pbk = dout("pbk", [128, 128])
    pbv = dout("pbv", [128, 128])
    sak = dout("sak", [16, 1024])
    sav = dout("sav", [16, 1024])
    sbk = dout("sbk", [16, 128])
    sbv = dout("sbv", [16, 128])

    NG = len(GROUPS)
    wsc = nc.dram_tensor("wsc", [NG, 128, 8192], BF16, kind="Internal").ap()
    ext_d = nc.dram_tensor("ext", [16, 512], F32, kind="Internal").ap()

    S = Sched(nc)
    with contextlib.ExitStack() as st:
        def sb(name, shape, dt):
            return st.enter_context(nc.sbuf_tensor("sb_" + name, shape, dt))

        xbuf = sb("xbuf", [128, 4, D], F32)
        uni = sb("uni", [128, 44 * 512], BF16)
        bufA = sb("bufA", [128, KC, 512], BF16)
        ka = sb("ka", [128, 8, 1024], BF16)
        va = sb("va", [128, 8, 16, 64], BF16)
        kbr = sb("kbr", [128, 2, 1024], BF16)
        vbr = sb("vbr", [128, 8, 2, 64], BF16)
        expB = sb("expB", [128, 16, 256], BF16)
        xstage = sb("xstage", [128, D], F32)
        xnb = sb("xnb", [128, D], BF16)
        wsl = [sb("wsl%d" % i, [128, KC, 512], BF16) for i in range(2)]
        recs = [sb("recs%d" % i, [128, 512], F32) for i in range(2)]
        recs2 = [sb("recsb%d" % i, [128, 512], F32) for i in range(2)]
        onesv = sb("onesv", [128, 64], BF16)
        onesv16 = sb("onesv16", [128, 64], BF16)
        maskB = sb("maskB", [128, 4, 64], BF16)
        esink2 = sb("esink2", [128, 8], F32)
        outst = [sb("outst%d" % i, [128, 512], F32) for i in range(2)]
        sgt = [sb("sgt%d" % i, [128, 512], BF16) for i in range(2)]
        identb = sb("identb", [128, 128], BF16)
        ident32 = sb("ident32", [128, 128], F32)
        ones32 = sb("ones32", [128, 128], F32)
        kbo = sb("kbo", [128, 128], F32)
        dummy = sb("dummyt", [128, 1], F32)
        onesb = sb("onesb", [128, 1], BF16)
        ropec = sb("ropec", [128, 33, 8], F32)
        ropes = sb("ropes", [128, 33, 8], F32)
        gains = sb("gains", [128, 3, 16], F32)
        cbias = sb("cbias", [128, 16], F32)
        negc = sb("negc", [128, 16], F32)
        esink = sb("esink", [128, 16], F32)
        epsb = sb("epsb", [128, 1], F32)
        negone = sb("negone", [128, 1], F32)
        stt = sb("stt", [128, 64], F32)
        stt2 = sb("stt2", [128, 32], F32)
        ropet = sb("ropet", [128, 4, 18, 8], F32)
        ext_sb = sb("ext_sb", [16, 512], F32)

        bufB = uni[:, 0:8192].rearrange("p (c n) -> p c n", n=512)
        qa = uni[:, 8192:12288].rearrange("p (c n) -> p c n", n=512)
        qb = uni[:, 12288:16384].rearrange("p (c n) -> p c n", n=512)
        NPT = 4
        PT = [uni[:, 16384 + 1024 * j:16384 + 1024 * (j + 1)].rearrange("p (e n) -> p e n", n=512) for j in range(NPT)]
        qkr_all = uni[:, 0:5120].rearrange("p (s n) -> p s n", n=1280)
        rst = uni[:, 5120:7424].bitcast(F32).rearrange("p (s h d) -> p s h d", h=18, d=16)
        sqc = [uni[:, 16384 + 512 * j:16384 + 512 * (j + 1)] for j in range(2)]
        actT = uni[:, :].rearrange("p (c n) -> p c n", n=512)
        stage32 = [xbuf[:, :, :].rearrange("p a b -> p (a b)").rearrange("p (k n) -> p k n", n=512),
                   uni[:, 0:16384].bitcast(F32).rearrange("p (k n) -> p k n", n=512)]
        bt = bufA[:, :, :].rearrange("p a b -> p (a b)").bitcast(F32).rearrange("p (h n) -> p h n", n=256)
        ystage = [uni[:, 0:4096].bitcast(F32), uni[:, 4096:8192].bitcast(F32)]
        gfin = xstage[:, :]

        def UNI(a, b):
            return [("uni", c) for c in range(a, b)]
        R_bufB = UNI(0, 16)
        R_stage = [[("xb", s) for s in range(4)], UNI(0, 32)]
        R_bufA = [("bufA", s) for s in range(4)]

        psS = st.enter_context(nc.psum_tensor("psS", [128, 4, 512], F32))
        psT = st.enter_context(nc.psum_tensor("psT", [128, 4, 512], F32))
        ps = [psS[:, b, :] for b in range(4)] + [psT[:, b, :] for b in range(4)]
        S_PAIRS = [(psS, 0), (psS, 2), (psT, 2)]
        srot = {"i": 0}

        def psb(b):
            return ps[b].bitcast(BF16)

        def cload(dst, src, res, **kw):
            S.op("sp", lambda e: e.dma_start(out=dst, in_=src, **kw), writes=[res], dma_key="c_" + res)

        cload(ident32[:], ident_d, "ident32")
        cload(ropec[:].rearrange("p a b -> p (a b)"), ropec_d, "ropec")
        cload(ropes[:].rearrange("p a b -> p (a b)"), ropes_d, "ropes")
        cload(gains[:, 0, :], norm_mix.rearrange("(k p) -> p k", p=128), "g0", allow_slow_non_contiguous=True)
        cload(gains[:, 1, :], norm_grp.rearrange("(k p) -> p k", p=128), "g1", allow_slow_non_contiguous=True)
        cload(gains[:, 2, :], norm_ffn.rearrange("(k p) -> p k", p=128), "g2", allow_slow_non_contiguous=True)
        cload(esink[:], sinks.partition_broadcast(128), "esink0")
        cload(cbias[:], rel_table[:, 256:257].rearrange("h o -> o h").partition_broadcast(128), "cbias",
              allow_slow_non_contiguous=True)
        cload(ext_sb[:, 0:257], rel_table, "ext_a")
        S.op("act", lambda e: e.activation(out=esink[:], in_=esink[:], func=AF.Exp), reads=["esink0"], writes=["esink"])
        S.op("dve", lambda e: e.tensor_scalar(out=negc[:], in0=cbias[:], scalar1=-1.0, scalar2=None, op0=ALU.mult),
             reads=["cbias"], writes=["negc"])
        S.op("dve", lambda e: e.tensor_copy(out=identb[:], in_=ident32[:]), reads=["ident32"], writes=["identb"])
        S.op("pool", lambda e: e.memset(ones32[:], 1.0), writes=["ones32"])
        S.op("pool", lambda e: e.memset(onesb[:], 1.0), writes=["onesb"])
        S.op("pool", lambda e: e.memset(epsb[:], EPS), writes=["epsb"])
        S.op("pool", lambda e: e.memset(negone[:], -1.0), writes=["negone"])
        S.op("pool", lambda e: e.memset(va[:].rearrange("p a b c -> p (a b c)"), 0.0), writes=[("va", s, hh) for s in range(8) for hh in range(2)])
        S.op("pool", lambda e: e.memset(vbr[:].rearrange("p a b c -> p (a b c)"), 0.0), writes=[("vb", s) for s in range(8)])
        S.op("pool", lambda e: e.memset(ka[:].rearrange("p a b -> p (a b)"), 0.0), writes=[("ka", s, hh) for s in range(8) for hh in range(2)])
        S.op("pool", lambda e: e.memset(kbr[:].rearrange("p a b -> p (a b)"), 0.0), writes=[("kb", s) for s in range(8)])
        S.op("pool", lambda e: e.memset(onesv[:], 1.0), writes=["onesv"])
        S.op("pool", lambda e: e.memset(onesv16[:], 0.0), writes=["onesv16a"])
        S.op("pool", lambda e: e.memset(onesv16[0:16, :], 1.0), reads=["onesv16a"], writes=["onesv16"])
        S.op("pool", lambda e: e.memset(maskB[:].rearrange("p a b -> p (a b)"), 1.0), writes=["maskBa"])
        S.op("pool", lambda e: e.memset(maskB[64:128, 0, :], 0.0), reads=["maskBa"], writes=["maskBb"])
        S.op("pool", lambda e: e.memset(maskB[0:64, 3, :], 0.0), reads=["maskBa"], writes=["maskBc"])
        R_maskB = ["maskBa", "maskBb", "maskBc"]
        sk2 = sinks.rearrange("o (i e) -> o i e", e=2)
        cload(esink2[0:64, :], sk2[:, :, 0].partition_broadcast(64), "esk0", allow_slow_non_contiguous=True)
        cload(esink2[64:128, :], sk2[:, :, 1].partition_broadcast(64), "esk1", allow_slow_non_contiguous=True)
        S.op("act", lambda e: e.activation(out=esink2[:], in_=esink2[:], func=AF.Exp), reads=["esk0", "esk1"], writes=["esink2"])
        S.op("dve", lambda e: e.tensor_copy(out=ext_sb[:, 257:512], in_=ext_sb[:, 256:257].to_broadcast([16, 255])),
             reads=["ext_a"], writes=["ext_b"])
        S.op("sp", lambda e: e.dma_start(out=ext_d, in_=ext_sb[:]), reads=["ext_a", "ext_b"], writes=["ext_d"], dma_key="ext")
        for k in range(128):
            q = "sp" if k % 2 == 0 else "pool"
            S.op(q, lambda e, k=k: e.dma_start(out=bt[k:k + 1, :, :], in_=ext_d[:, 128 - k:384 - k].unsqueeze(0)),
                 reads=["ext_d"], writes=[("bt", k)], dma_key="bt%d" % (k % 2))
        for h in range(16):
            S.op("act", lambda e, h=h: e.activation(out=expB[:, h, :], in_=bt[:, h, :], func=AF.Exp, bias=negc[:, h:h + 1]),
                 reads=[("bt", k) for k in range(128)] + ["negc"], writes=[("expB", h)])
        S.op("pool", lambda e: e.memset(expB[64:128, :, 0:64], 0.0), reads=[], writes=[("expB", h) for h in range(16)])
        R_expB = [("expB", h) for h in range(16)]

        GSEL = {"mix": 0, "out": 1, "ffn": 2}
        cvc = {"i": 0}
        SQ = [xbuf[:, 1, :].rearrange("p (k n) -> p k n", n=512), xbuf[:, 2, :].rearrange("p (k n) -> p k n", n=512),
              xbuf[:, 3, :].rearrange("p (k n) -> p k n", n=512), xstage[:, :].rearrange("p (k n) -> p k n", n=512)]
        R_SQ = [("xb", 1), ("xb", 2), ("xb", 3), "xstage"]
        sqr = {"i": 0}

        def WK(slot):
            return [("wk", slot, k, dc) for k in range(16) for dc in (0, 256)]

        def prep_load_q(wname, rows0, k0, k1, c0, ncols):
            i = sqr["i"] % 4
            sqr["i"] += 1
            src = W[wname].rearrange("(k p) n -> p k n", p=128)
            S.op("sp", lambda e: e.dma_start(out=SQ[i][:, 0:k1 - k0, 0:ncols], in_=src[:, rows0 + k0:rows0 + k1, c0:c0 + ncols]),
                 reads=[], writes=[R_SQ[i]], dma_key="sq%d" % i)
            return i

        def prep_convert(slot, k, dcol, i, kk, scol, n, gain):
            use_act = (cvc["i"] % 3 == 2)
            cvc["i"] += 1
            dst = wsl[slot][:, k, dcol:dcol + n]
            srcv = SQ[i][:, kk, scol:scol + n]
            wrs = [("wk", slot, k, dcol)] if n <= 256 else [("wk", slot, k, 0), ("wk", slot, k, 256)]
            rds = [R_SQ[i]]
            if gain is None:
                if use_act:
                    S.op("act", lambda e: e.copy(out=dst, in_=srcv), reads=rds, writes=wrs)
                else:
                    S.op("dve", lambda e: e.tensor_copy(out=dst, in_=srcv), reads=rds, writes=wrs)
            else:
                gs = GSEL[gain]
                if use_act:
                    S.op("act", lambda e: e.activation(out=dst, in_=srcv, func=AF.Copy, scale=gains[:, gs, k:k + 1]),
                         reads=rds + ["g%d" % gs], writes=wrs)
                else:
                    S.op("dve", lambda e: e.tensor_scalar(out=dst, in0=srcv, scalar1=gains[:, gs, k:k + 1], scalar2=None, op0=ALU.mult),
                         reads=rds + ["g%d" % gs], writes=wrs)

        def prep_store(gi, slot, nk):
            S.op("pool", lambda e: e.dma_start(out=wsc[gi, :, 0:nk * 512], in_=wsl[slot][:, 0:nk, :].rearrange("p k n -> p (k n)")),
                 reads=WK(slot), writes=[("wsc", gi)], dma_key="pw%d" % slot)

        def prep_single(gi, slot):
            g = GROUPS[gi]
            nk = g["nk"]
            (wname, c0, ncols, dcol) = g["parts"][0]
            for q in range((nk + 3) // 4):
                k0, k1 = 4 * q, min(nk, 4 * q + 4)
                i = prep_load_q(wname, g["rows0"], k0, k1, c0, ncols)
                for k in range(k0, k1):
                    prep_convert(slot, k, 0, i, k - k0, 0, ncols, g["gain"])
            prep_store(gi, slot, nk)

        def prep_gu_pair(gi, slot):
            i2 = int(GROUPS[gi]["name"][2:]) // 2
            for q in range(4):
                ia = prep_load_q("w_gate", 0, 4 * q, 4 * q + 4, 512 * i2, 512)
                ib = prep_load_q("w_up", 0, 4 * q, 4 * q + 4, 512 * i2, 512)
                for k in range(4 * q, 4 * q + 4):
                    for j in range(2):
                        sl = slot if j == 0 else 1 - slot
                        prep_convert(sl, k, 0, ia, k - 4 * q, 256 * j, 256, "ffn")
                        prep_convert(sl, k, 256, ib, k - 4 * q, 256 * j, 256, "ffn")
            prep_store(gi, slot, 16)
            prep_store(gi + 1, 1 - slot, 16)

        S.op("pool", lambda e: e.memset(dummy[:], 0.0), reads=[],
             writes=[("bt", k) for k in range(128)] + [("bufA", s_) for s_ in range(4)])

        wstate = {"n": 0, "fused": True}
        prepared = {}

        def load_group2(name):
            gi = GIDX[name]
            nk = GROUPS[gi]["nk"]
            slot = wstate["n"] % 2
            wstate["n"] += 1
            if wstate["fused"] and gi not in prepared:
                if name.startswith("gu"):
                    assert int(name[2:]) % 2 == 0
                    prep_gu_pair(gi, slot)
                    prepared[gi] = slot
                    prepared[gi + 1] = 1 - slot
                else:
                    prep_single(gi, slot)
                    prepared[gi] = None
                return slot
            if wstate["fused"] and prepared.get(gi) is not None:
                assert prepared[gi] == slot
                prepared[gi] = None
                return slot
            S.op("sp", lambda e, gi=gi, slot=slot, nk=nk: e.dma_start(
                out=wsl[slot][:, 0:nk, :].rearrange("p k n -> p (k n)"), in_=wsc[gi, :, 0:nk * 512]),
                reads=[("wsc", gi)], writes=WK(slot), dma_key="w%d" % slot)
            return slot

        def wres(slot):
            return WK(slot)

        bank_rr = {"i": 0}

        def nt_pre(td, s, from_stage):
            P = td.P
            c0 = (4 * s) % 16 + (16 if from_stage else 0)
            ssum, lnv, rstd = stt[:, c0:c0 + 1], stt[:, c0 + 1:c0 + 2], stt[:, c0 + 2:c0 + 3]
            rn = ("stt", c0)
            if from_stage:
                if td.kind == "prompt":
                    srcd = x_prompt[td.row0 + s * 128:td.row0 + s * 128 + 128, :]
                else:
                    srcd = x_sample[0:16, :]
                S.op("sp", lambda e: e.dma_start(out=xstage[0:P, :], in_=srcd), writes=["xstage"], dma_key="xs")
                src, rsrc = xstage[0:P, :], "xstage"
            else:
                src, rsrc = xbuf[0:P, s, :], ("xb", s)
            S.op("act", lambda e: e.activation(out=xnb[0:P, :], in_=src, func=AF.Square, accum_out=ssum[0:P, :]),
                 reads=[rsrc], writes=["xnb", (rn, 0)])
            S.op("act", lambda e: e.activation(out=lnv[0:P, :], in_=ssum[0:P, :], func=AF.Ln, scale=1.0 / D, bias=epsb[0:P, :]),
                 reads=[(rn, 0), "epsb"], writes=[(rn, 1)])
            S.op("act", lambda e: e.activation(out=rstd[0:P, :], in_=lnv[0:P, :], func=AF.Exp, scale=-0.5),
                 reads=[(rn, 1)], writes=[(rn, 2)])
            S.op("dve", lambda e: e.tensor_scalar(out=xnb[0:P, :], in0=src, scalar1=rstd[0:P, :], scalar2=None, op0=ALU.mult),
                 reads=[rsrc, (rn, 2)], writes=["xnb"])

        def nt_pe(td, s):
            P = td.P
            for half in range(2):
                b = 6 + half
                for j in range(8):
                    kc = 8 * half + j
                    S.op("pe", lambda e, b=b, j=j, kc=kc: e.transpose(out=psb(b)[:, j * 128:j * 128 + P],
                                                                     in_=xnb[0:P, kc * 128:(kc + 1) * 128],
                                                                     identity=identb[0:P, 0:P]),
                         reads=["xnb", "identb"], writes=[("ps", b)])
                src = psb(b).rearrange("p (j n) -> p j n", n=128)[:, :, 0:P]
                dst = bufA[:, 8 * half:8 * half + 8, s * 128:s * 128 + P]
                evac("act" if half == 0 else "dve", dst, src, [("ps", b)], [("bufA", s)])

        def norm_and_transpose(td, from_stage):
            for s in range(td.nsub):
                nt_pre(td, s, from_stage)
                nt_pe(td, s)

        def evac(eng, dst, src, reads, writes):
            if eng == "act":
                S.op("act", lambda e: e.copy(out=dst, in_=src), reads=reads, writes=writes)
            else:
                S.op("dve", lambda e: e.tensor_copy(out=dst, in_=src), reads=reads, writes=writes)

        def transposes_tok_to_feat(td, s, src_bf, nblk, dsts, bank, rsrc):
            P = td.P
            for j in range(nblk):
                S.op("pe", lambda e, j=j, P=P: e.transpose(out=psb(bank)[:, j * 128:j * 128 + P],
                                                          in_=src_bf[0:P, j * 128:(j + 1) * 128], identity=identb[0:P, 0:P]),
                     reads=rsrc + ["identb"], writes=[("ps", bank)])

        def tile_program(td, td_next):
            t, P, ntok = td.t, td.P, td.ntok
            ev = {"i": 0}

            def nexteng():
                if ev.get("dve_only"):
                    return "dve"
                ev["i"] += 1
                return "act" if ev["i"] % 2 == 0 else "dve"

            for s in range(td.nsub):
                if td.kind == "prompt":
                    src = x_prompt[td.row0 + s * 128:td.row0 + s * 128 + 128, :]
                else:
                    src = x_sample[0:16, :]
                wr = [("xb", s)]
                S.op("pool", lambda e, s=s, src=src, P=P: e.dma_start(out=xbuf[0:P, s, :], in_=src), writes=wr, dma_key="x%d" % s)

            if td.kind == "sample":
                for blk in range(4):
                    stg = outst[blk % 2]
                    for half in range(2):
                        S.op("sp", lambda e, blk=blk, half=half, stg=stg: e.dma_start(
                            out=stg[:, :], in_=cache_a_k[blk * 128:(blk + 1) * 128, half * 512:(half + 1) * 512]),
                            writes=[("outst", blk % 2)], dma_key="cl%d" % (blk % 2))
                        S.op("dve", lambda e, half=half, stg=stg: e.tensor_copy(out=xnb[:, half * 512:(half + 1) * 512], in_=stg[:, :]),
                             reads=[("outst", blk % 2)], writes=["xnb"])
                    for j in range(8):
                        S.op("pe", lambda e, j=j: e.transpose(out=psb(6)[:, j * 128:(j + 1) * 128], in_=xnb[:, j * 128:(j + 1) * 128],
                                                             identity=identb[:, :]),
                             reads=["xnb", "identb"], writes=[("ps", 6)])
                    S.op("act", lambda e, blk=blk: e.copy(out=ka[:, :, blk * 128:(blk + 1) * 128],
                                                         in_=psb(6).rearrange("p (j n) -> p j n", n=128)),
                         reads=[("ps", 6)], writes=[("ka", blk, 0), ("ka", blk, 1)])
                    for half in range(2):
                        S.op("sp", lambda e, blk=blk, half=half, stg=stg: e.dma_start(
                            out=stg[:, :], in_=cache_a_v[blk * 128:(blk + 1) * 128, half * 512:(half + 1) * 512]),
                            writes=[("outst", blk % 2)], dma_key="cl%d" % (blk % 2))
                        S.op("dve", lambda e, blk=blk, half=half, stg=stg: e.tensor_copy(
                            out=va[:, blk, 8 * half:8 * half + 8, :], in_=stg[:, :].rearrange("p (h d) -> p h d", d=64)),
                            reads=[("outst", blk % 2)], writes=[("va", blk, half)])
                stg = outst[0]
                S.op("sp", lambda e: e.dma_start(out=outst[0][:, 0:128], in_=cache_b_k), writes=[("outst", 0)], dma_key="cl0")
                S.op("sp", lambda e: e.dma_start(out=outst[0][:, 128:256], in_=cache_b_v), writes=[("outst", 0)], dma_key="cl0")
                kin = outst[0][:, 0:128].rearrange("p (g d) -> p g d", d=64)
                S.op("dve", lambda e, kin=kin: e.tensor_copy(out=xnb[:, 0:256].rearrange("p (g r d) -> p g r d", r=2, d=64)[:, :, 0, :], in_=kin),
                     reads=[("outst", 0)], writes=["xnb"])
                S.op("dve", lambda e, kin=kin: e.tensor_copy(out=xnb[:, 0:256].rearrange("p (g r d) -> p g r d", r=2, d=64)[:, :, 1, :], in_=kin),
                     reads=[("outst", 0)], writes=["xnb"])
                for j in range(2):
                    S.op("pe", lambda e, j=j: e.transpose(out=psb(6)[:, j * 128:(j + 1) * 128], in_=xnb[:, j * 128:(j + 1) * 128],
                                                         identity=identb[:, :]),
                         reads=["xnb", "identb"], writes=[("ps", 6)])
                S.op("act", lambda e: e.copy(out=kbr[:, :, 3 * 128:4 * 128], in_=psb(6)[:, 0:256].rearrange("p (j n) -> p j n", n=128)),
                     reads=[("ps", 6)], writes=[("kb", 3)])
                vin = outst[0][:, 128:256].rearrange("p (g d) -> p g d", d=64)
                S.op("dve", lambda e, vin=vin: e.tensor_copy(out=vbr[:, 3, :, :], in_=vin), reads=[("outst", 0)], writes=[("vb", 3)])
                S.op("pool", lambda e: e.memset(va[:, 4, :, :].rearrange("p a b -> p (a b)"), 0.0), writes=[("va", 4, 0), ("va", 4, 1)])
                S.op("pool", lambda e: e.memset(vbr[:, 4, :, :].rearrange("p a b -> p (a b)"), 0.0), writes=[("vb", 4)])
                S.op("pool", lambda e: e.memset(ka[:, :, 512:640], 0.0), writes=[("ka", 4, 0), ("ka", 4, 1)])
                S.op("pool", lambda e: e.memset(kbr[:, :, 512:640], 0.0), writes=[("kb", 4)])

            if td.kind == "sample":
                norm_and_transpose(td, False)

            rbufA = [("bufA", s) for s in range(td.nsub)]
            deferred = []
            gslot = {}

            def fm_unit(gname, oc, b):
                if gname not in gslot:
                    gslot[gname] = load_group2(gname)
                slot = gslot[gname]
                for kc in range(KC):
                    S.op("pe", lambda e, kc=kc: e.matmul(
                        ps[b][:, 0:ntok], lhsT=wsl[slot][:, kc, oc * 128:(oc + 1) * 128], rhs=bufA[:, kc, 0:ntok],
                        start=(kc == 0), stop=(kc == KC - 1)),
                        reads=wres(slot) + rbufA, writes=[("ps", b)])
                hh = 0 if gname[2] == "0" else 1
                pair = 4 * hh + oc
                if gname.startswith("qa"):
                    evac(nexteng(), qa[:, pair, 0:ntok], ps[b][:, 0:ntok], [("ps", b)], [("uni", 16 + pair)])
                else:
                    col0 = (t % 2) * 512
                    slots_w = [("ka", (4 * t + s) % 8, hh) for s in range(td.nsub)]
                    evac(nexteng(), ka[:, pair, col0:col0 + ntok], ps[b][:, 0:ntok], [("ps", b)], slots_w)

            def next_bank4():
                b = bank_rr["i"] % 4
                bank_rr["i"] += 1
                return b

            dbank = {"i": 0}

            def next_dbank():
                tns, b0 = S_PAIRS[srot["i"] % 3]
                srot["i"] += 1
                return b0 if tns is psS else 4 + b0

            for gname in ("qa0", "ka0"):
                for oc in range(4):
                    fm_unit(gname, oc, next_bank4())
            for gname in ("qa1", "ka1"):
                for oc in range(4):
                    deferred.append(lambda gname=gname, oc=oc: fm_unit(gname, oc, next_dbank()))

            tm_groups = ["qb0", "qb1", "kvb"] + (["ka0", "ka1"] if td.last else []) + ["va0", "va1"]
            co = {"i": 0}

            def out_store(dst_ap, src_ps, b, ncols):
                i = co["i"] % 2
                co["i"] += 1
                S.op("act", lambda e, i=i: e.copy(out=outst[i][0:P, 0:ncols], in_=src_ps), reads=[("ps", b)], writes=[("outst", i)])
                S.op("pool", lambda e, i=i: e.dma_start(out=dst_ap, in_=outst[i][0:P, 0:ncols]), reads=[("outst", i)], dma_key="co%d" % i)

            def tm_unit(gname, s, b):
                if (gname, "tm") not in gslot:
                    gslot[(gname, "tm")] = load_group2(gname)
                slot = gslot[(gname, "tm")]
                ncols = 256 if gname == "kvb" else 512
                for kc in range(KC):
                    S.op("pe", lambda e, b=b, slot=slot, kc=kc, s=s, ncols=ncols: e.matmul(
                        ps[b][0:P, 0:ncols], lhsT=bufA[:, kc, s * 128:s * 128 + P], rhs=wsl[slot][:, kc, 0:ncols],
                        start=(kc == 0), stop=(kc == KC - 1)),
                        reads=wres(slot) + [("bufA", s)], writes=[("ps", b)])
                blk = (4 * t + s) % 8
                RQK = R_bufB
                if gname in ("qb0", "qb1"):
                    g = int(gname[2])
                    acq = (s == 0 and g == 0)
                    S.op("act", lambda e, b=b, s=s, g=g: e.copy(out=qkr_all[0:P, s, g * 512:(g + 1) * 512], in_=ps[b][0:P, 0:512]),
                         reads=[("ps", b)] + ([] if acq else RQK), writes=[("qkq", s, g)] + (RQK if acq else []))
                    S.op("dve", lambda e, b=b, s=s, g=g: e.tensor_copy(
                        out=rst[0:P, s, 8 * g:8 * g + 8, :],
                        in_=ps[b][0:P, 0:512].rearrange("p (h d) -> p h d", d=64)[:, :, 0:16]),
                        reads=[("ps", b)] + RQK, writes=[("rst", s, g)])
                elif gname == "kvb":
                    kps = ps[b][0:P, 0:128].rearrange("p (g d) -> p g d", d=64)
                    S.op("dve", lambda e, s=s, kps=kps: e.tensor_copy(out=rst[0:P, s, 16:18, :], in_=kps[:, :, 0:16]),
                         reads=[("ps", b)] + RQK, writes=[("rst", s, 2)])
                    kd = qkr_all[0:P, s, 1024:1280].rearrange("p (g r d) -> p g r d", r=2, d=64)
                    S.op("act", lambda e, kd=kd, kps=kps: e.copy(out=kd[:, :, 0, :], in_=kps), reads=[("ps", b)] + RQK, writes=[("qkk", s, 0)])
                    S.op("act", lambda e, kd=kd, kps=kps: e.copy(out=kd[:, :, 1, :], in_=kps), reads=[("ps", b)] + RQK, writes=[("qkk", s, 1)])
                    vin = ps[b][0:P, 128:256].rearrange("p (g d) -> p g d", d=64)
                    S.op("act", lambda e, vin=vin, blk=blk: e.copy(out=vbr[0:P, blk, :, :], in_=vin),
                         reads=[("ps", b)], writes=[("vb", blk)])
                    want_out = td.last and (td.kind == "sample" or s == 3)
                    if want_out:
                        out_store(sbv[0:16, :] if td.kind == "sample" else pbv[:, :], ps[b][0:P, 128:256], b, 128)
                        S.op("act", lambda e, b=b: e.copy(out=kbo[0:P, :], in_=ps[b][0:P, 0:128]), reads=[("ps", b)], writes=["kbo"])
                    rq = [("rst", s, 0), ("rst", s, 1), ("rst", s, 2)]
                    x1, x2 = rst[0:P, s, :, 0:8], rst[0:P, s, :, 8:16]
                    jcol = (4 * t + s) if td.kind == "prompt" else 32
                    cs = ropec[0:P, jcol:jcol + 1, :].to_broadcast([P, 18, 8])
                    sn = ropes[0:P, jcol:jcol + 1, :].to_broadcast([P, 18, 8])
                    T = [ropet[0:P, i, :, :] for i in range(4)]
                    S.op("dve", lambda e, x1=x1, cs=cs, T=T: e.tensor_tensor(out=T[0], in0=x1, in1=cs, op=ALU.mult),
                         reads=rq + ["ropec"] + RQK, writes=[("ropet", 0)])
                    S.op("dve", lambda e, x2=x2, sn=sn, T=T: e.tensor_tensor(out=T[1], in0=x2, in1=sn, op=ALU.mult),
                         reads=rq + ["ropes"] + RQK, writes=[("ropet", 1)])
                    S.op("dve", lambda e, x2=x2, cs=cs, T=T: e.tensor_tensor(out=T[2], in0=x2, in1=cs, op=ALU.mult),
                         reads=rq + ["ropec"] + RQK, writes=[("ropet", 2)])
                    S.op("dve", lambda e, x1=x1, sn=sn, T=T: e.tensor_tensor(out=T[3], in0=x1, in1=sn, op=ALU.mult),
                         reads=rq + ["ropes"] + RQK, writes=[("ropet", 3)])
                    qv = qkr_all[0:P, s, 0:1024].rearrange("p (h d) -> p h d", d=64)
                    rqq = [("qkq", s, 0), ("qkq", s, 1)]
                    S.op("dve", lambda e, qv=qv, T=T: e.tensor_tensor(out=qv[:, :, 0:8], in0=T[0][:, 0:16, :], in1=T[1][:, 0:16, :], op=ALU.subtract),
                         reads=[("ropet", 0), ("ropet", 1)] + rqq + RQK, writes=[("qkq", s, 2)])
                    S.op("dve", lambda e, qv=qv, T=T: e.tensor_tensor(out=qv[:, :, 8:16], in0=T[2][:, 0:16, :], in1=T[3][:, 0:16, :], op=ALU.add),
                         reads=[("ropet", 2), ("ropet", 3)] + rqq + RQK, writes=[("qkq", s, 3)])
                    S.op("dve", lambda e, x1=x1, T=T: e.tensor_tensor(out=x1[:, 16:18, :], in0=T[0][:, 16:18, :], in1=T[1][:, 16:18, :], op=ALU.subtract),
                         reads=[("ropet", 0), ("ropet", 1), ("ropet", 3)] + RQK, writes=[("rstk", s, 0)])
                    S.op("dve", lambda e, x2=x2, T=T: e.tensor_tensor(out=x2[:, 16:18, :], in0=T[2][:, 16:18, :], in1=T[3][:, 16:18, :], op=ALU.add),
                         reads=[("ropet", 2), ("ropet", 3), ("ropet", 1)] + RQK, writes=[("rstk", s, 1)])
                    rkk = [("rstk", s, 0), ("rstk", s, 1)]
                    for r in range(2):
                        S.op("dve", lambda e, kd=kd, s=s, r=r: e.tensor_copy(out=kd[:, :, r, 0:16], in_=rst[0:P, s, 16:18, :]),
                             reads=rkk + [("qkk", s, r)] + RQK, writes=[("qkk", s, 2 + r)])
                    if want_out:
                        S.op("dve", lambda e, s=s: e.tensor_copy(out=kbo[0:P, :].rearrange("p (g d) -> p g d", d=64)[:, :, 0:16],
                                                                in_=rst[0:P, s, 16:18, :]),
                             reads=rkk + ["kbo"] + RQK, writes=["kbo2"])
                        dstk = sbk[0:16, :] if td.kind == "sample" else pbk[:, :]
                        S.op("pool", lambda e, dstk=dstk: e.dma_start(out=dstk, in_=kbo[0:P, :]), reads=["kbo", "kbo2"], dma_key="cok")
                    rqr = [("qkq", s, i) for i in range(4)] + [("qkk", s, i) for i in range(4)]
                    for j in range(8):
                        S.op("pe", lambda e, j=j, s=s: e.transpose(out=psb(6)[:, j * 128:j * 128 + P], in_=qkr_all[0:P, s, j * 128:(j + 1) * 128],
                                                                  identity=identb[0:P, 0:P]),
                             reads=rqr + ["identb"] + RQK, writes=[("ps", 6)])
                    S.op("act", lambda e, s=s: e.copy(out=qb[:, :, s * 128:s * 128 + P],
                                                     in_=psb(6).rearrange("p (j n) -> p j n", n=128)[:, :, 0:P]),
                         reads=[("ps", 6)], writes=[("uni", 24 + i) for i in range(8)] if s == 0 else [("qb", s)])
                    for j in range(2):
                        S.op("pe", lambda e, j=j, s=s: e.transpose(out=psb(7)[:, j * 128:j * 128 + P],
                                                                  in_=qkr_all[0:P, s, 1024 + j * 128:1024 + (j + 1) * 128],
                                                                  identity=identb[0:P, 0:P]),
                             reads=rqr + ["identb"] + RQK, writes=[("ps", 7)])
                    kcol = (t % 2) * 512 + s * 128
                    S.op("dve", lambda e, kcol=kcol: e.tensor_copy(out=kbr[:, :, kcol:kcol + P],
                                                                  in_=psb(7)[:, 0:256].rearrange("p (j n) -> p j n", n=128)[:, :, 0:P]),
                         reads=[("ps", 7)], writes=[("kb", blk)])
                elif gname in ("ka0", "ka1"):
                    g = int(gname[2])
                    dst = (sak[0:16, g * 512:(g + 1) * 512] if td.kind == "sample"
                           else pak[s * 128:(s + 1) * 128, g * 512:(g + 1) * 512])
                    out_store(dst, ps[b][0:P, 0:512], b, 512)
                else:
                    g = int(gname[2])
                    srcv = ps[b][0:P, 0:512].rearrange("p (h d) -> p h d", d=64)
                    dstv = va[0:P, blk, 8 * g:8 * g + 8, :]
                    evac(nexteng(), dstv, srcv, [("ps", b)], [("va", blk, g)])
                    if td.last:
                        dst = (sav[0:16, g * 512:(g + 1) * 512] if td.kind == "sample"
                               else pav[s * 128:(s + 1) * 128, g * 512:(g + 1) * 512])
                        out_store(dst, ps[b][0:P, 0:512], b, 512)


            for gname in tm_groups:
                for s in range(td.nsub):
                    if gname == "va1":
                        deferred.append(lambda gname=gname, s=s: tm_unit(gname, s, next_dbank()))
                    else:
                        b6 = bank_rr["i"] % 6
                        bank_rr["i"] += 1
                        tm_unit(gname, s, b6)

            rqb_all = [("uni", 24 + i) for i in range(8)] + [("qb", s) for s in range(1, td.nsub)]
            blocks = []
            pair_order = [("B", p_) for p_ in range(8)] + [("A", p_) for p_ in range(8)]
            ev["dve_only"] = True
            n_single = len(deferred)
            pair_ip = {kp: i_ for i_, kp in enumerate(pair_order)}
            for (kind, pair) in pair_order:
                if True:
                    if kind == "A":
                        kbs = range(max(0, 4 * t - 4), 4 * t + 4)
                        reach = 9
                    else:
                        kbs = range(max(0, 4 * t - 1), 4 * t + 4)
                        reach = 3
                    lst = []
                    for kb in kbs:
                        clo = max(0, 2 * kb - 8 * t)
                        chi = min(td.nqc - 1, 2 * kb + reach - 8 * t)
                        if clo > chi:
                            continue
                        lst.append((kb, clo, chi))
                    for i, (kb, clo, chi) in enumerate(lst):
                        blocks.append(dict(kind=kind, pair=pair, kb=kb, clo=clo, chi=chi,
                                           first=(i == 0), lastb=(i == len(lst) - 1)))
            nblk = len(blocks)

            def emit_S(bi):
                B_ = blocks[bi]
                kind, pair, kb, clo, chi = B_["kind"], B_["pair"], B_["kb"], B_["clo"], B_["chi"]
                nq = (chi - clo) * 64 + td.cq
                q0 = clo * 64
                slot = kb % 8
                stn, sbl = S_PAIRS[srot["i"] % 3]
                srot["i"] += 1
                sb0 = sbl if stn is psS else 4 + sbl
                B_["pt"] = bi % NPT
                B_["nq"], B_["q0"] = nq, q0
                for e_ in range(2):
                    lo, hi = 64 * e_, 64 * e_ + 64
                    if kind == "A":
                        S.op("pe", lambda e, lo=lo, hi=hi, e_=e_: e.matmul(ps[sb0 + e_][:, 0:nq], lhsT=ka[lo:hi, pair, slot * 128:(slot + 1) * 128],
                                                                          rhs=qa[lo:hi, pair, q0:q0 + nq], start=True, stop=True),
                             reads=[("ka", slot, pair // 4), ("uni", 16 + pair)], writes=[("ps", sb0 + e_)])
                    else:
                        kv = pair // 4
                        S.op("pe", lambda e, lo=lo, hi=hi, e_=e_, kv=kv: e.matmul(ps[sb0 + e_][:, 0:nq], lhsT=kbr[lo:hi, kv, slot * 128:(slot + 1) * 128],
                                                                                 rhs=qb[lo:hi, pair, q0:q0 + nq], start=True, stop=True),
                             reads=[("kb", slot)] + rqb_all, writes=[("ps", sb0 + e_)])
                pt = PT[B_["pt"]]
                rpts = [("uni", 32 + 2 * B_["pt"]), ("uni", 33 + 2 * B_["pt"])]
                m_lo = 8 * t + clo - 2 * kb
                m_hi = 8 * t + chi - 2 * kb
                S.op("act", lambda e: e.activation(out=pt[:, :, 0:nq], in_=stn[:, sbl:sbl + 2, 0:nq], func=AF.Exp, scale=0.125),
                     reads=[("ps", sb0), ("ps", sb0 + 1)], writes=rpts)
                if kind == "A":
                    if m_lo <= 3:
                        nb = min(3, m_hi) - m_lo + 1
                        ncol = (nb - 1) * 64 + td.cq
                        S.op("dve", lambda e: e.tensor_tensor(out=pt[:, :, 0:ncol], in0=pt[:, :, 0:ncol],
                                                              in1=expB[:, 2 * pair:2 * pair + 2, m_lo * 64:m_lo * 64 + ncol], op=ALU.mult),
                             reads=rpts + R_expB, writes=rpts)
                    if m_hi == 9:
                        c0 = (chi - clo) * 64
                        S.op("dve", lambda e: e.tensor_tensor(out=pt[:, :, c0:c0 + td.cq], in0=pt[:, :, c0:c0 + td.cq],
                                                              in1=maskB[:, 3:4, 0:td.cq].to_broadcast([128, 2, td.cq]), op=ALU.mult),
                             reads=rpts + R_maskB, writes=rpts)
                else:
                    mflat = maskB[:].rearrange("p a b -> p (a b)")[:, m_lo * 64:m_lo * 64 + nq]
                    S.op("dve", lambda e: e.tensor_tensor(out=pt[:, :, 0:nq], in0=pt[:, :, 0:nq],
                                                          in1=mflat.unsqueeze(1).to_broadcast([128, 2, nq]), op=ALU.mult),
                         reads=rpts + R_maskB, writes=rpts)

            def emit_PV(bi):
                B_ = blocks[bi]
                kind, pair, kb = B_["kind"], B_["pair"], B_["kb"]
                nq, q0 = B_["nq"], B_["q0"]
                slot = kb % 8
                pt = PT[B_["pt"]]
                rpts = [("uni", 32 + 2 * B_["pt"]), ("uni", 33 + 2 * B_["pt"])]
                ip_ = pair_ip[(kind, pair)]
                nb_ = 4
                db_ = nb_ + 1
                ov = onesv16 if (td.kind == "sample" and kb == 4) else onesv
                for e_ in range(2):
                    lo = 64 * e_
                    if kind == "A":
                        vsrc, rv = va[:, slot, 2 * pair + e_, :], ("va", slot, pair // 4)
                    else:
                        vsrc, rv = vbr[:, slot, pair // 4, :], ("vb", slot)
                    S.op("pe", lambda e, lo=lo, e_=e_, vsrc=vsrc: e.matmul(ps[nb_][lo:lo + 64, q0:q0 + nq], lhsT=vsrc, rhs=pt[:, e_, 0:nq],
                                                                         start=B_["first"], stop=B_["lastb"], skip_group_check=True),
                         reads=[rv] + rpts, writes=[("ps", nb_)])
                for e_ in range(2):
                    lo = 64 * e_
                    S.op("pe", lambda e, lo=lo, e_=e_: e.matmul(ps[db_][lo:lo + 64, q0:q0 + nq], lhsT=ov[:, :], rhs=pt[:, e_, 0:nq],
                                                               start=B_["first"], stop=B_["lastb"], skip_group_check=True),
                         reads=rpts + ["onesv", "onesv16"], writes=[("ps", db_)])
                if B_["lastb"]:
                    finalize(kind, pair, nb_, db_)

            fin_ctr = {"i": 0}

            def finalize(kind, pair, nb_, db_):
                chunk = pair + (8 if kind == "B" else 0)
                ri = fin_ctr["i"] % 2
                fin_ctr["i"] += 1
                rc = recs[ri]
                rc2 = recs2[ri]
                if kind == "B":
                    S.op("act", lambda e: e.activation(out=rc[:, 0:ntok], in_=ps[db_][:, 0:ntok], func=AF.Ln, bias=esink2[:, pair:pair + 1]),
                         reads=[("ps", db_), "esink2"], writes=[("recs", ri)])
                else:
                    S.op("act", lambda e: e.activation(out=rc[:, 0:ntok], in_=ps[db_][:, 0:ntok], func=AF.Ln),
                         reads=[("ps", db_)], writes=[("recs", ri)])
                S.op("act", lambda e: e.activation(out=rc2[:, 0:ntok], in_=rc[:, 0:ntok], func=AF.Exp, scale=-1.0),
                     reads=[("recs", ri)], writes=[("recs2", ri)])
                S.op("dve", lambda e: e.tensor_tensor(out=bufB[:, chunk, 0:ntok], in0=ps[nb_][:, 0:ntok], in1=rc2[:, 0:ntok], op=ALU.mult),
                     reads=[("ps", nb_), ("recs2", ri)], writes=[("uni", chunk)])
                if deferred:
                    deferred.pop(0)()

            LOOK = 2
            for bi in range(nblk + LOOK):
                if bi < nblk:
                    emit_S(bi)
                if bi - LOOK >= 0:
                    emit_PV(bi - LOOK)

            ev["dve_only"] = False
            first_mm = True
            for c in range(16):
                sq = sqc[c % 2]
                rs = ("uni", 32 + c % 2)
                S.op("act", lambda e, c=c, sq=sq: e.activation(out=sq[:, 0:ntok], in_=bufB[:, c, 0:ntok], func=AF.Square),
                     reads=[("uni", c)], writes=[rs])
                for s in range(td.nsub):
                    col = (c // 8) * 4 + s
                    S.op("pe", lambda e, sq=sq, s=s, col=col, fm=first_mm: e.matmul(
                        ps[6][0:P, col:col + 1], lhsT=sq[:, s * 128:s * 128 + P], rhs=onesb[:, 0:1],
                        start=fm, stop=(c == 15 and s == td.nsub - 1), skip_group_check=True),
                        reads=[rs, "onesb"], writes=[("ps", 6)])
                    first_mm = False
            lnr = stt[:, 48:56]
            rr = stt[:, 56:64]
            if td.nsub < 4:
                pass
            for g in range(2):
                S.op("act", lambda e, g=g: e.activation(out=lnr[0:P, 4 * g:4 * g + td.nsub], in_=ps[6][0:P, 4 * g:4 * g + td.nsub],
                                                       func=AF.Ln, scale=1.0 / 1024, bias=epsb[0:P, :]),
                     reads=[("ps", 6), "epsb"], writes=[("lnr", g)])
                S.op("act", lambda e, g=g: e.activation(out=rr[0:P, 4 * g:4 * g + td.nsub], in_=lnr[0:P, 4 * g:4 * g + td.nsub],
                                                       func=AF.Exp, scale=-0.5),
                     reads=[("lnr", g)], writes=[("rr", g)])
            def ffn_norm_sub(s):
                rstd = stt2[:, 16 + 4 * s + 2:16 + 4 * s + 3]
                S.op("dve", lambda e: e.tensor_scalar(out=xnb[0:P, :], in0=xbuf[0:P, s, :], scalar1=rstd[0:P, :],
                                                      scalar2=None, op0=ALU.mult),
                     reads=[("xb", s), (("stt2n", s), 2)], writes=["xnb"])
                nt_pe(td, s)

            for cg in range(4):
                slot = load_group2("wo%d" % cg)
                for s in range(td.nsub):
                    if cg == 3 and s >= 2:
                        ffn_norm_sub(s - 2)
                    pb = 2 * (bank_rr["i"] % 3)
                    bank_rr["i"] += 1
                    for half in range(2):
                        for k in range(8):
                            kc = 8 * half + k
                            S.op("pe", lambda e, pb=pb, half=half, kc=kc, k=k, s=s, slot=slot: e.matmul(
                                ps[pb + half][0:P, 0:512], lhsT=bufB[:, kc, s * 128:s * 128 + P], rhs=wsl[slot][:, kc, 0:512],
                                start=(k == 0), stop=(k == 7)),
                                reads=wres(slot) + [("uni", kc)], writes=[("ps", pb + half)])
                    for half in range(2):
                        S.op("dve", lambda e, pb=pb, half=half, s=s, cg=cg: e.scalar_tensor_tensor(
                            out=xbuf[0:P, s, cg * 512:(cg + 1) * 512], in0=ps[pb + half][0:P, 0:512],
                            scalar=rr[0:P, 4 * half + s:4 * half + s + 1], in1=xbuf[0:P, s, cg * 512:(cg + 1) * 512],
                            op0=ALU.mult, op1=ALU.add),
                            reads=[("ps", pb + half), ("rr", half), ("xb", s)], writes=[("xb", s)])
                    sgj = sgt[(4 * cg + s) % 2]
                    S.op("act", lambda e, s=s, cg=cg, sgj=sgj: e.activation(
                        out=sgj[0:P, :], in_=xbuf[0:P, s, cg * 512:(cg + 1) * 512], func=AF.Square,
                        accum_out=stt2[0:P, 4 * s + cg:4 * s + cg + 1]),
                        reads=[("xb", s)], writes=[("sgt", (4 * cg + s) % 2), ("stt2", s, cg)])
                    if cg == 3:
                        c0 = 16 + 4 * s
                        ssum, lnv, rstd = stt2[:, c0:c0 + 1], stt2[:, c0 + 1:c0 + 2], stt2[:, c0 + 2:c0 + 3]
                        rn2 = ("stt2n", s)
                        S.op("dve", lambda e, s=s, ssum=ssum: e.reduce_sum(out=ssum[0:P, :], in_=stt2[0:P, 4 * s:4 * s + 4],
                                                                          axis=mybir.AxisListType.X),
                             reads=[("stt2", s, c) for c in range(4)], writes=[(rn2, 0)])
                        S.op("act", lambda e, ssum=ssum, lnv=lnv: e.activation(out=lnv[0:P, :], in_=ssum[0:P, :], func=AF.Ln,
                                                                              scale=1.0 / D, bias=epsb[0:P, :]),
                             reads=[(rn2, 0), "epsb"], writes=[(rn2, 1)])
                        S.op("act", lambda e, rstd=rstd, lnv=lnv: e.activation(out=rstd[0:P, :], in_=lnv[0:P, :], func=AF.Exp, scale=-0.5),
                             reads=[(rn2, 1)], writes=[(rn2, 2)])

            for s in range(max(0, td.nsub - 2), td.nsub):
                ffn_norm_sub(s)

            for j in range(22):
                slot = load_group2("gu%d" % j)
                for f in range(2):
                    ffc = 2 * j + f
                    pb = 2 * (bank_rr["i"] % 4)
                    bank_rr["i"] += 1
                    for which in range(2):
                        for kc in range(KC):
                            c0 = 256 * which + 128 * f
                            S.op("pe", lambda e, pb=pb, which=which, kc=kc, c0=c0, slot=slot: e.matmul(
                                ps[pb + which][:, 0:ntok], lhsT=wsl[slot][:, kc, c0:c0 + 128], rhs=bufA[:, kc, 0:ntok],
                                start=(kc == 0), stop=(kc == KC - 1)),
                                reads=wres(slot) + rbufA, writes=[("ps", pb + which)])
                    sg = sgt[ffc % 2]
                    S.op("act", lambda e, pb=pb, sg=sg: e.activation(out=sg[:, 0:ntok], in_=ps[pb][:, 0:ntok], func=AF.Silu),
                         reads=[("ps", pb)], writes=[("sgt", ffc % 2)])
                    S.op("dve", lambda e, pb=pb, sg=sg, ffc=ffc: e.tensor_tensor(out=actT[:, ffc, 0:ntok], in0=ps[pb + 1][:, 0:ntok],
                                                                               in1=sg[:, 0:ntok], op=ALU.mult),
                         reads=[("ps", pb + 1), ("sgt", ffc % 2)], writes=[("uni", ffc), ("qb", 1), ("qb", 2), ("qb", 3)] if ffc in range(24, 32) else [("uni", ffc)])

            R_gfin = ["xstage"]
            gf_done = {"v": False}

            def load_gfin():
                gf_done["v"] = True
                S.op("pool", lambda e: e.dma_start(out=gfin[:, :], in_=norm_final.partition_broadcast(128)), writes=R_gfin, dma_key="gf")

            gidx = 0
            if td_next is not None:
                nt_pre(td_next, 0, True)
            for cg in range(4):
                banks = [0, 1, 2, 3] if cg % 2 == 0 else [4, 5, 2, 3]
                for rg in range(4):
                    slot = load_group2("dn%d_%d" % (cg, rg))
                    for i in range(11):
                        ffc = rg * 11 + i
                        for s in range(td.nsub):
                            S.op("pe", lambda e, s=s, i=i, ffc=ffc, slot=slot, bk=banks[s]: e.matmul(
                                ps[bk][0:P, 0:512], lhsT=actT[:, ffc, s * 128:s * 128 + P], rhs=wsl[slot][:, i, 0:512],
                                start=(ffc == 0), stop=(ffc == NFF - 1)),
                                reads=wres(slot) + [("uni", ffc)], writes=[("ps", banks[s])])
                    if td_next is not None:
                        sn_ = gidx - 2
                        if 0 <= sn_ < td_next.nsub:
                            nt_pe(td_next, sn_)
                            if sn_ + 1 < td_next.nsub:
                                nt_pre(td_next, sn_ + 1, True)
                            else:
                                load_gfin()
                    gidx += 1
                for s in (2, 3, 0, 1):
                    if s >= td.nsub:
                        continue
                    S.op("dve", lambda e, s=s, cg=cg, bk=banks[s]: e.tensor_tensor(
                        out=xbuf[0:P, s, cg * 512:(cg + 1) * 512], in0=ps[bk][0:P, 0:512],
                        in1=xbuf[0:P, s, cg * 512:(cg + 1) * 512], op=ALU.add),
                        reads=[("ps", banks[s]), ("xb", s)], writes=[("xb", s)])
            if not gf_done["v"]:
                load_gfin()
            for s in range(td.nsub):
                c0 = 32 + 4 * s
                ssum, lnv, rstd = stt[:, c0:c0 + 1], stt[:, c0 + 1:c0 + 2], stt[:, c0 + 2:c0 + 3]
                rn = ("stt", c0)
                ys = ystage[s % 2]
                ry = [("uni", c) for c in range(8 * (s % 2), 8 * (s % 2) + 8)]
                S.op("act", lambda e, s=s, ssum=ssum, ys=ys: e.activation(out=ys[0:P, :], in_=xbuf[0:P, s, :], func=AF.Square,
                                                                         accum_out=ssum[0:P, :]),
                     reads=[("xb", s)], writes=ry + [(rn, 0)])
                S.op("act", lambda e, ssum=ssum, lnv=lnv: e.activation(out=lnv[0:P, :], in_=ssum[0:P, :], func=AF.Ln,
                                                                      scale=1.0 / D, bias=epsb[0:P, :]),
                     reads=[(rn, 0), "epsb"], writes=[(rn, 1)])
                S.op("act", lambda e, rstd=rstd, lnv=lnv: e.activation(out=rstd[0:P, :], in_=lnv[0:P, :], func=AF.Exp, scale=-0.5),
                     reads=[(rn, 1)], writes=[(rn, 2)])
                S.op("dve", lambda e, s=s, rstd=rstd, ys=ys: e.scalar_tensor_tensor(
                    out=ys[0:P, :], in0=xbuf[0:P, s, :], scalar=rstd[0:P, :], in1=gfin[0:P, :], op0=ALU.mult, op1=ALU.mult),
                    reads=[("xb", s), (rn, 2)] + R_gfin, writes=ry)
                dst = (y_prompt[td.row0 + s * 128:td.row0 + s * 128 + 128, :] if td.kind == "prompt" else y_sample[0:16, :])
                S.op("pool", lambda e, ys=ys, dst=dst: e.dma_start(out=dst, in_=ys[0:P, :]), reads=ry,
                     dma_key="y%d" % (s % 2))

        ptiles = [TileDesc(t, 4, 128, 8, 64, t == NT - 1, "prompt", t * 512) for t in range(NT)]
        if with_sample:
            tile_program(TileDesc(1, 1, 16, 1, 16, True, "sample", 0), None)
        wstate["fused"] = False
        assert len(prepared) == NG or not with_sample
        norm_and_transpose(ptiles[0], True)
        for i, td in enumerate(ptiles):
            tile_program(td, ptiles[i + 1] if i + 1 < len(ptiles) else None)

        with nc.allow_low_precision(reason="bf16 matmul operands, fp32 accumulation"):
            S.emit()
    return nc


def _consts():
    half = 8
    inv_freq = (np.float32(500000.0) ** (-(np.arange(half, dtype=np.float32) * np.float32(2.0) / np.float32(16)))).astype(np.float32)
    pos = np.zeros((128, 33), np.float32)
    for j in range(32):
        pos[:, j] = 128 * j + np.arange(128)
    pos[:, 32] = 2048 + np.arange(128)
    ang = (pos[:, :, None] * inv_freq[None, None, :]).astype(np.float32)
    return (np.eye(128, dtype=np.float32), np.cos(ang).astype(np.float32).reshape(128, 33 * 8),
            np.sin(ang).astype(np.float32).reshape(128, 33 * 8))


_NC_CACHE = {}


def make_in_map(i, inp, NT=8):
    ident, rc, rs = _consts()
    f = lambda a: np.ascontiguousarray(a, dtype=np.float32)
    return {
        "x_prompt": f(inp["x_prompt"][i][:NT * 512]),
        "x_sample": f(inp["x_sample"][i]),
        "cache_a_k": f(inp["cache_a_k"][0, i].reshape(512, 1024)),
        "cache_a_v": f(inp["cache_a_v"][0, i].reshape(512, 1024)),
        "cache_b_k": f(inp["cache_b_k"][0, i].reshape(128, 128)),
        "cache_b_v": f(inp["cache_b_v"][0, i].reshape(128, 128)),
        "w_in": f(inp["w_in"][0]),
        "w_out": f(inp["w_out"][0]),
        "w_gate": f(inp["w_gate"][0]),
        "w_up": f(inp["w_up"][0]),
        "w_down": f(inp["w_down"][0]),
        "norm_mix": f(inp["norm_mix"][0]),
        "rel_table": f(inp["rel_table"][0]),
        "sinks": f(inp["sinks"][0].reshape(1, 16)),
        "norm_grp": f(np.concatenate([inp["norm_grp_a"][0], inp["norm_grp_b"][0]])),
        "norm_ffn": f(inp["norm_ffn"][0]),
        "norm_final": f(inp["norm_final"].reshape(1, D)),
        "ident": ident, "ropec": rc, "ropes": rs,
    }


def kernel(**inputs):
    inp = {k: np.asarray(v) for k, v in inputs.items()}
    if "nc" not in _NC_CACHE:
        _NC_CACHE["nc"] = build(8, True)
    nc = _NC_CACHE["nc"]
    in_maps = [make_in_map(i, inp) for i in range(N_CORES)]
    res = run_bass_kernel_spmd(nc, in_maps, core_ids=list(range(N_CORES)))
    R = res.results

    def stack(name, shape):
        return np.stack([np.asarray(R[i][name], dtype=np.float32).reshape(shape) for i in range(N_CORES)])

    y_prompt = stack("y_prompt", (4096, D))
    y_sample = stack("y_sample", (16, D))
    pak = stack("pak", (512, 16, 64))[None]
    pav = stack("pav", (512, 16, 64))[None]
    pbk = stack("pbk", (128, 2, 64))[None]
    pbv = stack("pbv", (128, 2, 64))[None]
    sak = stack("sak", (16, 16, 64))[None]
    sav = stack("sav", (16, 16, 64))[None]
    sbk = stack("sbk", (16, 2, 64))[None]
    sbv = stack("sbv", (16, 2, 64))[None]
    return (y_prompt, y_sample, pak, pav, pbk, pbv, sak, sav, sbk, sbv)
```
